# Optimizing a Trainium2 kernel written in Bass

```python
import math
import jax, jax.numpy as jnp
from jax import lax
import numpy as np

D_MODEL = 1024
BATCH = 8
SEQ = 4096
DEPTH = 2

DN_HEADS = 8
DN_DK = 64
DN_DV = 64
DN_CONV = 5
DN_CHUNK = 64
ATTN_HEADS = 8
ATTN_KV_HEADS = 2
ATTN_GROUP = ATTN_HEADS // ATTN_KV_HEADS
ATTN_HD = 64
WINDOW = 128
ATTN_BLOCK = 128
ATTN_SPAN = ATTN_BLOCK + 2 * WINDOW
D_FF = 4 * D_MODEL
NORM_EPS = 1e-6

DN_QK = DN_HEADS * DN_DK
DN_V = DN_HEADS * DN_DV
DN_CONV_CH = 2 * DN_QK + DN_V
ATTN_Q = ATTN_HEADS * ATTN_HD
ATTN_KV = ATTN_KV_HEADS * ATTN_HD
IN_SPLITS = (DN_QK, DN_QK, DN_V, DN_V, 4 * DN_HEADS, ATTN_Q, ATTN_KV, ATTN_KV, 2 * D_MODEL)
IN_COLS = 2 * DN_QK + 2 * DN_V + 4 * DN_HEADS + ATTN_Q + 2 * ATTN_KV + 2 * D_MODEL

kernel_name = "hybrid_gdn_swa_bidir_encoder"


def rmsnorm(t, w):
    tf = t.astype(jnp.float32)
    y = tf * lax.rsqrt(jnp.mean(tf * tf, axis=-1, keepdims=True) + NORM_EPS)
    return (y * w.astype(jnp.float32)).astype(t.dtype)


def l2norm(t):
    tf = t.astype(jnp.float32)
    return tf * lax.rsqrt(jnp.sum(tf * tf, axis=-1, keepdims=True) + NORM_EPS)


def split_cols(h, sizes):
    out, o = [], 0
    for s in sizes:
        out.append(h[..., o:o + s])
        o += s
    return out


def short_conv(u, w):
    k = w.shape[0]
    return lax.conv_general_dilated(
        u, w[:, None, :].astype(u.dtype), window_strides=(1,),
        padding=[(k // 2, k // 2)], dimension_numbers=("NWC", "WIO", "NWC"),
        feature_group_count=u.shape[-1])


def chunk_gated_delta(q, k, v, g, beta):
    b_, s_, h_, dk = q.shape
    dv = v.shape[-1]
    n = s_ // DN_CHUNK
    f32 = jnp.float32

    def chunks(t):
        return jnp.moveaxis(t.astype(f32).reshape((b_, n, DN_CHUNK) + t.shape[2:]), 2, 3)

    q = chunks(q) * (dk ** -0.5)
    k = chunks(k)
    v = chunks(v)
    g = chunks(g)
    beta = chunks(beta)
    gc = jnp.cumsum(g, axis=-1)
    idx = jnp.arange(DN_CHUNK)
    incl = idx[:, None] >= idx[None, :]
    strict = idx[:, None] > idx[None, :]
    diff = gc[..., :, None] - gc[..., None, :]
    decay = jnp.exp(jnp.where(incl, diff, -jnp.inf))
    kb = k * beta[..., None]
    lmat = jnp.where(strict, jnp.einsum("bnhik,bnhjk->bnhij", kb, k) * decay, 0.0)
    rhs = jnp.concatenate([v * beta[..., None], kb * jnp.exp(gc)[..., None]], axis=-1)
    sol = lax.linalg.triangular_solve(lmat, rhs, left_side=True, lower=True, unit_diagonal=True)
    u = sol[..., :dv]
    w = sol[..., dv:]
    attn = jnp.einsum("bnhik,bnhjk->bnhij", q, k) * decay

    def step(state, inp):
        q_i, k_i, u_i, w_i, gc_i, a_i = inp
        v_new = u_i - jnp.einsum("bhck,bhkv->bhcv", w_i, state)
        o_i = (jnp.einsum("bhck,bhkv->bhcv", q_i * jnp.exp(gc_i)[..., None], state)
               + jnp.einsum("bhij,bhjv->bhiv", a_i, v_new))
        g_last = gc_i[..., -1]
        state = (state * jnp.exp(g_last)[..., None, None]
                 + jnp.einsum("bhck,bhcv->bhkv",
                              k_i * jnp.exp(g_last[..., None] - gc_i)[..., None], v_new))
        return state, o_i

    xs = tuple(jnp.moveaxis(t, 1, 0) for t in (q, k, u, w, gc, attn))
    state0 = jnp.zeros((b_, h_, dk, dv), f32)
    _, o = lax.scan(step, state0, xs)
    o = jnp.moveaxis(jnp.moveaxis(o, 0, 1), 2, 3)
    return o.reshape(b_, s_, h_, dv)


def window_attention(q, k, v, sink):
    b_, s_, _, hd = q.shape
    nb = s_ // ATTN_BLOCK
    qb = q.reshape(b_, nb, ATTN_BLOCK, ATTN_KV_HEADS, ATTN_GROUP, hd)
    pad = ((0, 0), (WINDOW, WINDOW), (0, 0), (0, 0))
    kp = jnp.pad(k, pad)
    vp = jnp.pad(v, pad)
    kidx = (jnp.arange(nb) * ATTN_BLOCK)[:, None] + jnp.arange(ATTN_SPAN)[None, :]
    kb = kp[:, kidx]
    vb = vp[:, kidx]
    scores = jnp.einsum("bnqhgd,bnkhd->bnhgqk", qb, kb).astype(jnp.float32) * (hd ** -0.5)
    tpos = (jnp.arange(nb) * ATTN_BLOCK)[:, None] + jnp.arange(ATTN_BLOCK)[None, :]
    spos = kidx - WINDOW
    dist = jnp.abs(tpos[:, :, None] - spos[:, None, :])
    valid = (dist <= WINDOW) & (spos[:, None, :] >= 0) & (spos[:, None, :] < s_)
    slopes = jnp.exp2(-8.0 * jnp.arange(1, ATTN_HEADS + 1, dtype=jnp.float32) / ATTN_HEADS)
    slopes = slopes.reshape(ATTN_KV_HEADS, ATTN_GROUP, 1, 1)
    bias = -slopes * dist[:, None, None].astype(jnp.float32)
    scores = jnp.where(valid[:, None, None], scores + bias, -jnp.inf)
    sink_l = jnp.broadcast_to(
        sink.astype(jnp.float32).reshape(ATTN_KV_HEADS, ATTN_GROUP, 1, 1),
        scores.shape[:-1] + (1,))
    probs = jax.nn.softmax(jnp.concatenate([scores, sink_l], axis=-1), axis=-1)[..., :-1]
    out = jnp.einsum("bnhgqk,bnkhd->bnqhgd", probs.astype(v.dtype), vb)
    return out.reshape(b_, s_, ATTN_HEADS * hd)


def setup_inputs(seed: int = 0) -> dict:
    key = jax.random.key(seed)
    ks = jax.random.split(key, 20)
    f = jnp.float32

    def nrm(k, shape, scale):
        return jax.random.normal(k, shape, f) * scale

    x = nrm(ks[0], (BATCH, SEQ, D_MODEL), 1.0)
    w_in = nrm(ks[1], (DEPTH, D_MODEL, IN_COLS), D_MODEL ** -0.5)
    conv_w = nrm(ks[2], (DEPTH, DN_CONV, DN_CONV_CH), DN_CONV ** -0.5)
    a_log = jnp.log(jax.random.uniform(ks[3], (DEPTH, 2, DN_HEADS), f, 0.5, 4.0))
    dt = jnp.exp(jax.random.uniform(ks[4], (DEPTH, 2, DN_HEADS), f,
                                    math.log(1e-3), math.log(1e-1)))
    dt_bias = dt + jnp.log(-jnp.expm1(-dt))
    dn_norm_w = 1.0 + nrm(ks[5], (DEPTH, DN_DV), 0.01)
    attn_sink = nrm(ks[6], (DEPTH, ATTN_HEADS), 0.5)
    w_up_a = nrm(ks[7], (DEPTH, DN_V, D_MODEL), DN_V ** -0.5)
    w_up_b = nrm(ks[8], (DEPTH, ATTN_Q, D_MODEL), ATTN_Q ** -0.5)
    w_out = nrm(ks[9], (DEPTH, D_MODEL, D_MODEL), D_MODEL ** -0.5)
    norm_mix_pre = 1.0 + nrm(ks[10], (DEPTH, D_MODEL), 0.01)
    norm_mix_post = 1.0 + nrm(ks[11], (DEPTH, D_MODEL), 0.01)
    norm_mlp_pre = 1.0 + nrm(ks[12], (DEPTH, D_MODEL), 0.01)
    norm_mlp_post = 1.0 + nrm(ks[13], (DEPTH, D_MODEL), 0.01)
    w_mlp_in = nrm(ks[14], (DEPTH, D_MODEL, D_FF), D_MODEL ** -0.5)
    w_mlp_out = nrm(ks[15], (DEPTH, D_FF, D_MODEL), D_FF ** -0.5)
    return {"x": x, "w_in": w_in, "conv_w": conv_w, "a_log": a_log, "dt_bias": dt_bias,
            "dn_norm_w": dn_norm_w, "attn_sink": attn_sink, "w_up_a": w_up_a,
            "w_up_b": w_up_b, "w_out": w_out, "norm_mix_pre": norm_mix_pre,
            "norm_mix_post": norm_mix_post, "norm_mlp_pre": norm_mlp_pre,
            "norm_mlp_post": norm_mlp_post, "w_mlp_in": w_mlp_in, "w_mlp_out": w_mlp_out}


def reference(x, w_in, conv_w, a_log, dt_bias, dn_norm_w, attn_sink, w_up_a, w_up_b, w_out,
              norm_mix_pre, norm_mix_post, norm_mlp_pre, norm_mlp_post, w_mlp_in, w_mlp_out):
    b_, s_, _ = x.shape
    flip = lambda t: jnp.flip(t, axis=1)
    for l in range(DEPTH):
        h = rmsnorm(x, norm_mix_pre[l])
        p = h @ w_in[l]
        dq, dk_, dv_, dz, dgates, aq, ak, av, bgates = split_cols(p, IN_SPLITS)

        qkv = jax.nn.silu(short_conv(jnp.concatenate([dq, dk_, dv_], axis=-1), conv_w[l]))
        dq, dk_, dv_ = split_cols(qkv, (DN_QK, DN_QK, DN_V))
        dq = l2norm(dq.reshape(b_, s_, DN_HEADS, DN_DK))
        dk_ = l2norm(dk_.reshape(b_, s_, DN_HEADS, DN_DK))
        dv_ = dv_.reshape(b_, s_, DN_HEADS, DN_DV).astype(jnp.float32)
        a_f, a_b, be_f, be_b = split_cols(dgates.astype(jnp.float32), (DN_HEADS,) * 4)
        decay_rate = jnp.exp(a_log[l].astype(jnp.float32))
        dtb = dt_bias[l].astype(jnp.float32)
        g_f = -decay_rate[0] * jax.nn.softplus(a_f + dtb[0])
        g_b = -decay_rate[1] * jax.nn.softplus(a_b + dtb[1])
        o_fwd = chunk_gated_delta(dq, dk_, dv_, g_f, jax.nn.sigmoid(be_f))
        o_bwd = flip(chunk_gated_delta(flip(dq), flip(dk_), flip(dv_), flip(g_b),
                                       flip(jax.nn.sigmoid(be_b))))
        o_a = (o_fwd + o_bwd).astype(x.dtype)
        o_a = rmsnorm(o_a, dn_norm_w[l]) * jax.nn.silu(dz.reshape(b_, s_, DN_HEADS, DN_DV))
        y_a = o_a.reshape(b_, s_, DN_V) @ w_up_a[l]

        o_b = window_attention(aq.reshape(b_, s_, ATTN_HEADS, ATTN_HD),
                               ak.reshape(b_, s_, ATTN_KV_HEADS, ATTN_HD),
                               av.reshape(b_, s_, ATTN_KV_HEADS, ATTN_HD), attn_sink[l])
        y_b = o_b @ w_up_b[l]

        g_ma, g_mb = split_cols(bgates, (D_MODEL, D_MODEL))
        mix = (jax.nn.sigmoid(g_ma) * y_a + jax.nn.sigmoid(g_mb) * y_b) @ w_out[l]
        x = x + rmsnorm(mix, norm_mix_post[l])

        h = rmsnorm(x, norm_mlp_pre[l])
        u = jnp.square(jax.nn.relu(h @ w_mlp_in[l]))
        x = x + rmsnorm(u @ w_mlp_out[l], norm_mlp_post[l])
    return x
```

```python
import contextlib
import numpy as np
import concourse.bass as bass
import concourse.mybir as mybir

F32 = mybir.dt.float32
BF16 = mybir.dt.bfloat16
ALU = mybir.AluOpType
AF = mybir.ActivationFunctionType
AX = mybir.AxisListType


class Reg:
    __slots__ = ("name", "writers", "readers", "excl")

    def __init__(self, name="", excl=False):
        self.name = name
        self.excl = excl
        self.writers = []
        self.readers = []


class Ctx:
    ENG = ("pe", "act", "dve", "pool", "sp")

    def __init__(self, nc, n_dma_sems=24):
        self.nc = nc
        self.es = contextlib.ExitStack()
        self.eng = {"pe": nc.tensor, "act": nc.scalar, "dve": nc.vector,
                    "pool": nc.gpsimd, "sp": nc.sync}
        self.phase_id = 0
        self.n_dma_sems = n_dma_sems
        self.dma_sems = [self.es.enter_context(nc.semaphore(f"dq{i}")) for i in range(n_dma_sems)]
        self.dma_tot = [0] * n_dma_sems
        self.dma_rr = 0
        self.semsets = [{e: self.es.enter_context(nc.semaphore(f"s{k}_{e}")) for e in self.ENG}
                        for k in range(3)]
        self._new_sems()

    def _new_sems(self):
        self.sem = self.semsets[self.phase_id % 3]
        self.cnt = {e: 0 for e in self.ENG}
        self.seen = {e: {} for e in self.ENG}
        self.pending = {e: [] for e in self.ENG}

    def sb(self, name, shape, dtype):
        self.uid = getattr(self, "uid", 0) + 1
        return self.es.enter_context(self.nc.sbuf_tensor(f"sb{self.uid}_{name}", list(shape), dtype))

    def ps(self, name, shape, dtype=F32):
        self.uid = getattr(self, "uid", 0) + 1
        return self.es.enter_context(self.nc.psum_tensor(f"ps{self.uid}_{name}", list(shape), dtype))

    def _need(self, e, deps):
        need = {}
        for d in deps:
            if d is None:
                continue
            src, val, ph = d
            if isinstance(src, str) and ph != self.phase_id:
                continue
            if isinstance(val, list):
                if val[0] is None:
                    raise RuntimeError(f"dependency on unsignaled op of {src}")
                val = val[0]
            if val > need.get(src, 0):
                need[src] = val
        for src, val in need.items():
            if self.seen[e].get(src, 0) >= val:
                continue
            s = self.sem[src] if isinstance(src, str) else self.dma_sems[src[1]]
            self.eng[e].wait_ge(s, val)
            self.seen[e][src] = val

    def _collect(self, reads, writes, e, pe_accum=False, adds=()):
        deps = []
        for r in reads:
            deps.extend(r.writers)
            if r.excl:
                deps.extend(t for t in r.readers if t[0] != e)
        for w in writes:
            for t in w.writers:
                if not (pe_accum and t[0] == "pe"):
                    deps.append(t)
            deps.extend(w.readers)
        for w in adds:
            deps.extend(w.readers)
        return deps

    def _commit(self, tok, reads, writes, adds):
        for r in reads:
            r.readers.append(tok)
        for w in writes:
            w.writers = [tok]
            w.readers = []
        for w in adds:
            w.writers.append(tok)

    @contextlib.contextmanager
    def scope(self):
        old = self.es
        self.es = contextlib.ExitStack()
        try:
            yield
        finally:
            self.barrier()
            self.es.close()
            self.es = old

    def op(self, e, fn, reads=(), writes=(), sig=True, pe_accum=False, adds=()):
        deps = self._collect(reads, writes, e, pe_accum, adds)
        self._need(e, deps)
        inst = fn()
        if sig:
            self.cnt[e] += 1
            inst.then_inc(self.sem[e], 1)
            tok = (e, self.cnt[e], self.phase_id)
            for p in self.pending[e]:
                p[0] = self.cnt[e]
            self.pending[e] = []
        else:
            cell = [None]
            self.pending[e].append(cell)
            tok = (e, cell, self.phase_id)
        self._commit(tok, reads, writes, adds)
        return inst

    def dma(self, q, out, in_, reads=(), writes=(), adds=(), **kw):
        deps = self._collect(reads, writes, q, False, adds)
        i = self.dma_rr
        self.dma_rr = (self.dma_rr + 1) % self.n_dma_sems
        if self.dma_tot[i] > 0:
            deps.append((("d", i), self.dma_tot[i], 0))
        self._need(q, deps)
        self.dma_tot[i] += 16
        inst = self.eng[q].dma_start(out=out, in_=in_, **kw)
        inst.then_inc(self.dma_sems[i], 16)
        tok = (("d", i), self.dma_tot[i], 0)
        self._commit(tok, reads, writes, adds)
        return inst

    def drain_dmas(self, e="sp"):
        deps = [(("d", i), t, 0) for i, t in enumerate(self.dma_tot) if t > 0]
        self._need(e, deps)

    def barrier(self):
        for e in self.ENG:
            deps = [(("d", i), t, 0) for i, t in enumerate(self.dma_tot) if t > 0]
            for o in self.ENG:
                if o != e and self.cnt[o] > 0:
                    deps.append((o, self.cnt[o], self.phase_id))
            self._need(e, deps)
        old_seen = self.seen
        self.phase_id += 1
        self._new_sems()
        nxt = self.semsets[(self.phase_id + 1) % 3]
        for e in self.ENG:
            self.eng[e].sem_clear(nxt[e])
        for e in self.ENG:
            self.seen[e] = {k: v for k, v in old_seen[e].items() if not isinstance(k, str)}

    def close(self):
        self.es.close()


import ml_dtypes
from concourse.bass_utils import run_bass_kernel_spmd

F32R = mybir.dt.float32r
D = 1024
INC = 4896
DFF = 4096
C_Q, C_K, C_V, C_Z, C_G, C_AQ, C_AK, C_AV, C_GA, C_GB = 0, 512, 1024, 1536, 2048, 2080, 2592, 2720, 2848, 3872
EPS = 1e-6
NEG = -30000.0


class Rot:
    def __init__(self, c, name, n, shape, dtype, psum=False):
        self.items = []
        for i in range(n):
            t = c.ps(f"{name}{i}", shape, dtype) if psum else c.sb(f"{name}{i}", shape, dtype)
            self.items.append((t, Reg(f"{name}{i}", excl=psum)))
        self.i = 0

    def next(self):
        it = self.items[self.i]
        self.i = (self.i + 1) % len(self.items)
        return it


def host_consts():
    p = np.arange(128)[:, None]
    f = np.arange(128)[None, :]
    same = (p // 64) == (f // 64)
    cf = np.zeros((128, 34, 128), np.float32)
    cf[:, 0] = np.where(same & (f >= p), 0.0, NEG)
    cf[:, 1] = np.where(same & (p > f), 0.0, NEG)
    cf[:, 2] = np.where(same & (p >= f), 0.0, NEG)
    cf[:, 3] = np.where(same & (f > p), 0.0, NEG)
    cf[:, 4] = (same & (p <= f)).astype(np.float32)
    cf[:, 5] = (same & (p >= f)).astype(np.float32)
    cf[:, 6] = same.astype(np.float32)
    cf[:, 7] = (p == f).astype(np.float32)
    cf[:, 8, 0] = (np.arange(128) < 64)
    cf[:, 8, 1] = (np.arange(128) >= 64)
    for k in range(6):
        b = 1 << k
        pat = ((p // (2 * b)) == (f // (2 * b))) & ((p // b) != (f // b))
        cf[:, 10 + k] = (pat & (p > f)).astype(np.float32)
        cf[:, 16 + k] = (pat & (p < f)).astype(np.float32)
        cf[:, 22 + k] = -cf[:, 10 + k]
        cf[:, 28 + k] = -cf[:, 16 + k]
    slopes = np.exp2(-8.0 * np.arange(1, 9, dtype=np.float32) / 8)
    cb = np.zeros((128, 3, 8, 128), np.float32)
    for kt, off in enumerate((-128, 0, 128)):
        dist = (f - (p + off))
        valid = np.abs(dist) <= 128
        for h in range(8):
            cb[:, kt, h, :] = np.where(valid, -8.0 * slopes[h] * np.abs(dist), NEG * 8)
    return cf, cb.astype(ml_dtypes.bfloat16)


class Prog:
    def __init__(self, S, depth, dbg=(), stop=None):
        self.S, self.depth, self.dbg, self.stop = S, depth, set(dbg), stop
        nc = self.nc = bass.Bass("TRN2", target_bir_lowering=False)
        self.nph = 0
        dt = lambda n, sh, d=F32: nc.dram_tensor(n, list(sh), d, kind="ExternalInput").ap()
        L = depth
        self.x = dt("x", [S, D])
        self.w_in = dt("w_in", [L, D, INC])
        self.conv_w = dt("conv_w", [L, 5, 1536])
        self.a_log = dt("a_log", [L, 16])
        self.dt_bias = dt("dt_bias", [L, 16])
        self.dn_norm_w = dt("dn_norm_w", [L, 64])
        self.attn_sink = dt("attn_sink", [L, 8])
        self.w_up_a = dt("w_up_a", [L, 512, D])
        self.w_up_b = dt("w_up_b", [L, 512, D])
        self.w_out = dt("w_out", [L, D, D])
        self.n_mix_pre = dt("norm_mix_pre", [L, D])
        self.n_mix_post = dt("norm_mix_post", [L, D])
        self.n_mlp_pre = dt("norm_mlp_pre", [L, D])
        self.n_mlp_post = dt("norm_mlp_post", [L, D])
        self.w1 = dt("w_mlp_in", [L, D, DFF])
        self.w2 = dt("w_mlp_out", [L, DFF, D])
        self.cf_d = dt("cf", [128, 34, 128])
        self.cb_d = dt("cb", [128, 3, 8, 128], BF16)
        self.out = nc.dram_tensor("out", [S, D], F32, kind="ExternalOutput").ap()
        self.c = Ctx(nc)
        self.build()
        self.c.close()

    def dump(self, name, ap, reg, dtype=F32):
        if name not in self.dbg:
            return
        shape = list(ap.shape)
        d = self.nc.dram_tensor("dbg_" + name, shape, dtype, kind="ExternalOutput").ap()
        self.c.dma("sp", d, ap, reads=[reg])

    def scr(self, name, shape, dtype=F32):
        kind = "ExternalOutput" if name in self.dbg else "Internal"
        return self.nc.dram_tensor(name, list(shape), dtype, kind=kind).ap()

    def build(self):
        c, nc, S = self.c, self.nc, self.S
        self.cf = c.sb("cf", [128, 34, 128], F32)
        self.Rcf = Reg("cf")
        c.dma("sp", self.cf[:], self.cf_d[:, :, :], writes=[self.Rcf])
        self.identb = c.sb("identb", [128, 128], BF16)
        self.Ridb = Reg("identb")
        c.op("dve", lambda: nc.vector.tensor_copy(out=self.identb[:], in_=self.cf[:, 7, :]),
             reads=[self.Rcf], writes=[self.Ridb])
        self.qkvT = self.scr("qkvT", [1536, S])
        self.aqT = self.scr("aqT", [512, S], BF16)
        self.akT = self.scr("akT", [128, S], BF16)
        self.sgT = self.scr("sgT", [2048, S])
        self.ztok = self.scr("ztok", [S, 512])
        self.gtok = self.scr("gtok", [S, 32])
        self.avtok = self.scr("avtok", [S, 128], BF16)
        NT = S // 128
        self.ts_d = self.scr("ts_d", [S, 80])
        self.gcT_d = self.scr("gcT_d", [16, S])
        self.egl_d = self.scr("egl_d", [NT * 32])
        self.qT_d = self.scr("qT_d", [512, S], BF16)
        self.kT_d = self.scr("kT_d", [512, S], BF16)
        self.ktok_d = self.scr("ktok_d", [S, 512])
        self.vtok_d = self.scr("vtok_d", [S, 512])
        self.oaT_d = self.scr("oaT_d", [512, S], BF16)
        self.odir_d = [self.scr("ofwd_d", [S, 512]), self.scr("obwd_d", [S, 512])]
        self.obT_d = self.scr("obT_d", [512, S], BF16)
        self.uT_d = self.scr("uT_d", [DFF, S], BF16)
        self.xa = self.scr("xa", [S, D])
        self.xb = self.scr("xb", [S, D])
        for l in range(self.depth):
            xin = self.x if l == 0 else self.xb
            x1 = self.xa
            x2 = self.out if l == self.depth - 1 else self.xb
            for fn in (lambda: self.p1(l, xin), lambda: self.p2a(l), lambda: self.p2b(l), lambda: self.p3(l),
                       lambda: self.p4(l), lambda: self.p5(l, xin, x1), lambda: self.p6a(l, x1),
                       lambda: self.p6b(l, x1, x2)):
                if self.phase(fn):
                    return

    def phase(self, fn):
        with self.c.scope():
            fn()
        self.nph += 1
        return self.stop is not None and self.nph >= self.stop

    def rstd_of(self, ss_ap, out_ap, Rss, n):
        c, nc = self.c, self.nc
        c.op("act", lambda: nc.scalar.activation(out=out_ap, in_=ss_ap, func=AF.Sqrt,
                                                 bias=self.epsb[:, 0:1], scale=1.0 / n),
             reads=[Rss, self.Reps], writes=[Rss])
        c.op("dve", lambda: nc.vector.reciprocal(out=out_ap, in_=out_ap), reads=[Rss], writes=[Rss])

    def load_wcast(self, dst, src, Rlist_append, ncols):
        c = self.c
        for c0 in range(0, ncols, 2048):
            c1 = min(ncols, c0 + 2048)
            r = Reg("w")
            c.dma("pool", dst[:, c0:c1], src[:, c0:c1], writes=[r])
            Rlist_append.append(r)

    def norm_to_hT(self, X, r0, l, gammaT, hT, RhT, tt, xts, ss, Rss, k, pT, RpT):
        c, nc = self.c, self.nc
        xt, Rxt = xts.next()
        c.dma("sp", xt[:], X[r0:r0 + 128, :], writes=[Rxt])
        junk, Rj = self.junk
        c.op("act", lambda: nc.scalar.activation(out=junk[:], in_=xt[:], func=AF.Square,
                                                 accum_out=ss[:, 2 * k:2 * k + 1]),
             reads=[Rxt], writes=[Rj, Rss])
        self.rstd_of(ss[:, 2 * k:2 * k + 1], ss[:, 2 * k + 1:2 * k + 2], Rss, D)
        xn, Rxn = self.xn
        c.op("dve", lambda: nc.vector.tensor_scalar(out=xn[:], in0=xt[:], scalar1=ss[:, 2 * k + 1:2 * k + 2],
                                                    scalar2=None, op0=ALU.mult),
             reads=[Rxt, Rss], writes=[Rxn])
        for kc in range(8):
            c.op("pe", lambda: nc.tensor.transpose(pT[:, kc, :], xn[:, kc * 128:(kc + 1) * 128], self.identb[:]),
                 reads=[Rxn, self.Ridb], writes=[RpT], pe_accum=True, sig=(kc == 7))
        c.op("dve", lambda: nc.vector.tensor_tensor(
            out=hT[:, :, tt * 128:(tt + 1) * 128], in0=pT[:],
            in1=gammaT[:, :].unsqueeze(2).to_broadcast([128, 8, 128]), op=ALU.mult),
            reads=[RpT, self.Rgam], adds=[RhT])

    def p1(self, l, X):
        c, nc, S = self.c, self.nc, self.S
        NB = S // 512
        self.epsb = c.sb("epsb", [128, 1], F32)
        self.Reps = Reg()
        c.op("pool", lambda: nc.gpsimd.memset(self.epsb[:], EPS), writes=[self.Reps])
        wb = c.sb("wb", [128, 8, INC], BF16)
        Rwb = [[] for _ in range(8)]
        for c0 in range(0, INC, 2048):
            c1 = min(INC, c0 + 2048)
            for kc in range(8):
                r = Reg("w")
                c.dma("pool", wb[:, kc, c0:c1], self.w_in[l, kc * 128:(kc + 1) * 128, c0:c1], writes=[r])
                Rwb[kc].append(r)
        gammaT = c.sb("gammaT", [128, 8], F32)
        self.Rgam = Reg()
        with nc.allow_non_contiguous_dma(reason="tiny gamma transpose"):
            c.dma("sp", gammaT[:], self.n_mix_pre.rearrange("l (c p) -> l p c", p=128)[l], writes=[self.Rgam])
        xts = Rot(c, "xt", 2, [128, D], F32)
        self.junk = (c.sb("junk", [128, D], BF16), Reg())
        self.xn = (c.sb("xn", [128, D], BF16), Reg())
        ss = c.sb("ss", [128, 2 * NB * 4], F32)
        Rss = Reg()
        c.op("pool", lambda: nc.gpsimd.memset(ss[:], 0.0), writes=[Rss])
        pT = c.ps("pT", [128, 8, 128], BF16)
        RpT = Reg("pT", excl=True)
        hTs = Rot(c, "hT", 2, [128, 8, 512], BF16)
        pss = Rot(c, "pf", 4, [128, 512], F32, psum=True)
        stg = Rot(c, "stg", 4, [128, 512], F32)
        stgb = Rot(c, "stgb", 3, [128, 512], BF16)
        stgs = Rot(c, "stgs", 2, [128, 32], F32)
        evi = [0]

        def evac_copy(dst, src, rd, wr):
            e = ("dve", "act")[evi[0] % 2]
            evi[0] += 1
            if e == "dve":
                c.op("dve", lambda: nc.vector.tensor_copy(out=dst, in_=src), reads=rd, writes=wr)
            else:
                c.op("act", lambda: nc.scalar.copy(out=dst, in_=src), reads=rd, writes=wr)

        hbuf = [hTs.next() for _ in range(2)]

        def emit_norm(bb, tt):
            hTn, RhTn = hbuf[bb % 2]
            if tt == 0:
                RhTn.writers = []
            self.norm_to_hT(X, bb * 512 + tt * 128, l, gammaT, hTn, RhTn, tt, xts, ss, Rss, bb * 4 + tt, pT, RpT)

        for tt in range(4):
            emit_norm(0, tt)
        for b in range(NB):
            hT, RhT = hbuf[b % 2]
            t0 = b * 512
            fm = []
            for i in range(12):
                fm.append((C_Q + i * 128, 128, "copy32", self.qkvT[i * 128:(i + 1) * 128, t0:t0 + 512]))
            for i in range(4):
                fm.append((C_AQ + i * 128, 128, "copy16", self.aqT[i * 128:(i + 1) * 128, t0:t0 + 512]))
            fm.append((C_AK, 128, "copy16", self.akT[:, t0:t0 + 512]))
            for i in range(16):
                fm.append((C_GA + i * 128, 128, "sig", self.sgT[i * 128:(i + 1) * 128, t0:t0 + 512]))
            for fi, (c0, m, kind, dst) in enumerate(fm):
                if b + 1 < NB and fi in (6, 12, 18, 24):
                    emit_norm(b + 1, (fi - 6) // 6)
                ps, Rps = pss.next()
                for kc in range(8):
                    c.op("pe", lambda: nc.tensor.matmul(ps[0:m, :], lhsT=wb[:, kc, c0:c0 + m], rhs=hT[:, kc, :],
                                                        start=(kc == 0), stop=(kc == 7)),
                         reads=Rwb[kc] + [RhT], writes=[Rps], pe_accum=True, sig=(kc == 7))
                if kind == "copy16":
                    st, Rst = stgb.next()
                    evac_copy(st[0:m, :], ps[0:m, :], [Rps], [Rst])
                elif kind == "copy32":
                    st, Rst = stg.next()
                    evac_copy(st[0:m, :], ps[0:m, :], [Rps], [Rst])
                else:
                    st, Rst = stg.next()
                    c.op("act", lambda: nc.scalar.activation(out=st[0:m, :], in_=ps[0:m, :], func=AF.Sigmoid),
                         reads=[Rps], writes=[Rst])
                c.dma("sp", dst, st[0:m, :], reads=[Rst])
            for tt in range(4):
                r0 = t0 + tt * 128
                for (c0, n, kind, dst) in ((C_Z, 512, "copy32", self.ztok[r0:r0 + 128, :]),
                                           (C_G, 32, "small", self.gtok[r0:r0 + 128, :]),
                                           (C_AV, 128, "copy16", self.avtok[r0:r0 + 128, :])):
                    ps, Rps = pss.next()
                    for kc in range(8):
                        c.op("pe", lambda: nc.tensor.matmul(ps[:, 0:n], lhsT=hT[:, kc, tt * 128:(tt + 1) * 128],
                                                            rhs=wb[:, kc, c0:c0 + n], start=(kc == 0), stop=(kc == 7)),
                             reads=Rwb[kc] + [RhT], writes=[Rps], pe_accum=True, sig=(kc == 7))
                    if kind == "copy32":
                        st, Rst = stg.next()
                    elif kind == "small":
                        st, Rst = stgs.next()
                    else:
                        st, Rst = stgb.next()
                    evac_copy(st[:, 0:n], ps[:, 0:n], [Rps], [Rst])
                    c.dma("sp", dst, st[:, 0:n], reads=[Rst])


def _p2a(self, l):
    c, nc, S = self.c, self.nc, self.S
    NT = S // 128
    cf = self.cf
    G = c.sb("G", [128, NT, 32], F32); RG = Reg()
    c.dma("sp", G[:], self.gtok.rearrange("(t p) c -> p t c", p=128), writes=[RG])
    DTB = c.sb("DTB", [128, 16], F32); ALG = c.sb("ALG", [128, 16], F32); Rc = Reg()
    c.dma("sp", DTB[:], self.dt_bias[l:l + 1, :].partition_broadcast(128), adds=[Rc])
    c.dma("sp", ALG[:], self.a_log[l:l + 1, :].partition_broadcast(128), adds=[Rc])
    one = c.sb("one", [128, 1], F32); R1 = Reg()
    c.op("pool", lambda: nc.gpsimd.memset(one[:], 1.0), writes=[R1])
    NR = c.sb("NR", [128, 16], F32); RNR = Reg()
    c.op("act", lambda: nc.scalar.activation(out=NR[:], in_=ALG[:], func=AF.Exp), reads=[Rc], writes=[RNR])
    c.op("dve", lambda: nc.vector.tensor_scalar(out=NR[:], in0=NR[:], scalar1=-1.0, scalar2=None, op0=ALU.mult),
         reads=[RNR], writes=[RNR])
    sh = [128, NT, 16]
    bc = lambda t: t[:, :].unsqueeze(1).to_broadcast(sh)
    A = c.sb("A", sh, F32); RA = Reg()
    AB = c.sb("AB", sh, F32); RAB = Reg()
    GG = c.sb("GG", sh, F32); RGG = Reg()
    BE = c.sb("BE", sh, F32); RBE = Reg()
    LNB = c.sb("LNB", sh, F32); RLNB = Reg()
    c.op("dve", lambda: nc.vector.tensor_tensor(out=A[:], in0=G[:, :, 0:16], in1=bc(DTB), op=ALU.add),
         reads=[RG, Rc], writes=[RA])
    c.op("act", lambda: nc.scalar.activation(out=AB[:], in_=A[:], func=AF.Abs), reads=[RA], writes=[RAB])
    c.op("act", lambda: nc.scalar.activation(out=AB[:], in_=AB[:], func=AF.Exp, scale=-1.0), reads=[RAB], writes=[RAB])
    c.op("act", lambda: nc.scalar.activation(out=AB[:], in_=AB[:], func=AF.Ln, bias=one[:, 0:1]),
         reads=[RAB, R1], writes=[RAB])
    c.op("dve", lambda: nc.vector.tensor_scalar_max(out=A[:], in0=A[:], scalar1=0.0), reads=[RA], writes=[RA])
    c.op("dve", lambda: nc.vector.tensor_tensor(out=A[:], in0=A[:], in1=AB[:], op=ALU.add), reads=[RA, RAB], writes=[RA])
    c.op("dve", lambda: nc.vector.tensor_tensor(out=GG[:], in0=A[:], in1=bc(NR), op=ALU.mult),
         reads=[RA, RNR], writes=[RGG])
    c.op("act", lambda: nc.scalar.activation(out=BE[:], in_=G[:, :, 16:32], func=AF.Sigmoid), reads=[RG], writes=[RBE])
    c.op("act", lambda: nc.scalar.activation(out=LNB[:], in_=BE[:], func=AF.Ln), reads=[RBE], writes=[RLNB])
    bank = lambda n: c.ps(n, [128, 512], F32)
    GCp = bank("GCp")[:, 0:NT * 16].rearrange("p (t c) -> p t c", c=16); RGCp = Reg("GCp", excl=True)
    GLp = bank("GLp")[:, 0:NT * 16].rearrange("p (t c) -> p t c", c=16); RGLp = Reg("GLp", excl=True)
    EGp = bank("EGp")[0:16, 0:NT * 2].rearrange("p (t c) -> p t c", c=2); REGp = Reg("EGp", excl=True)
    for t in range(NT):
        last = (t == NT - 1)
        c.op("pe", lambda: nc.tensor.matmul(GCp[:, t, 0:8], lhsT=cf[:, 4, :], rhs=GG[:, t, 0:8], start=True, stop=True),
             reads=[RGG, self.Rcf], writes=[RGCp], pe_accum=True, sig=False)
        c.op("pe", lambda: nc.tensor.matmul(GCp[:, t, 8:16], lhsT=cf[:, 5, :], rhs=GG[:, t, 8:16], start=True, stop=True),
             reads=[RGG, self.Rcf], writes=[RGCp], pe_accum=True, sig=last)
        c.op("pe", lambda: nc.tensor.matmul(GLp[:, t, :], lhsT=cf[:, 6, :], rhs=GG[:, t, :], start=True, stop=True),
             reads=[RGG, self.Rcf], writes=[RGLp], pe_accum=True, sig=last)
        c.op("pe", lambda: nc.tensor.matmul(EGp[:, t, :], lhsT=GG[:, t, :], rhs=cf[:, 8, 0:2], start=True, stop=True),
             reads=[RGG, self.Rcf], writes=[REGp], pe_accum=True, sig=last)
    TS = c.sb("TS", [128, NT, 80], F32); RTS = Reg()
    TMP = c.sb("TMP", sh, F32); RTMP = Reg()
    c.op("act", lambda: nc.scalar.copy(out=TS[:, :, 0:16], in_=GCp), reads=[RGCp], adds=[RTS])
    c.op("dve", lambda: nc.vector.tensor_tensor(out=TS[:, :, 16:32], in0=GCp, in1=LNB[:], op=ALU.add),
         reads=[RGCp, RLNB], adds=[RTS])
    c.op("pool", lambda: nc.gpsimd.tensor_copy(out=TS[:, :, 32:48], in_=BE[:]), reads=[RBE], adds=[RTS])
    c.op("act", lambda: nc.scalar.activation(out=TMP[:], in_=GCp, func=AF.Exp), reads=[RGCp], writes=[RTMP])
    c.op("dve", lambda: nc.vector.tensor_tensor(out=TS[:, :, 48:64], in0=TMP[:], in1=BE[:], op=ALU.mult),
         reads=[RTMP, RBE], adds=[RTS])
    TM2 = c.sb("TM2", sh, F32); RTM2 = Reg()
    c.op("act", lambda: nc.scalar.copy(out=TM2[:], in_=GCp), reads=[RGCp], writes=[RTM2])
    c.op("dve", lambda: nc.vector.tensor_tensor(out=TM2[:], in0=GLp, in1=TM2[:], op=ALU.subtract),
         reads=[RGLp, RTM2], writes=[RTM2])
    c.op("act", lambda: nc.scalar.activation(out=TS[:, :, 64:80], in_=TM2[:], func=AF.Exp), reads=[RTM2], adds=[RTS])
    c.dma("sp", self.ts_d.rearrange("(t p) c -> p t c", p=128), TS[:], reads=[RTS])
    EGs = c.sb("EGs", [16, NT, 2], F32); REGs = Reg()
    c.op("act", lambda: nc.scalar.activation(out=EGs[:], in_=EGp, func=AF.Exp), reads=[REGp], writes=[REGs])
    with nc.allow_non_contiguous_dma(reason="tiny egl transpose"):
        for d in range(2):
            for ch in range(2):
                c.dma("sp", self.egl_d.rearrange("(t d c h) -> d c h t", d=2, c=2, h=8)[d, ch],
                      EGs[d * 8:(d + 1) * 8, :, ch], reads=[REGs])
    gcs = c.sb("gcs", [8, 2, S], F32); Rgcs = Reg()
    pgs = Rot(c, "pg", 2, [8, 512], F32, psum=True)
    for d in range(2):
        for g4 in range(0, NT, 4):
            ps, Rps = pgs.next()
            n = min(4, NT - g4)
            for i in range(n):
                t = g4 + i
                c.op("pe", lambda: nc.tensor.matmul(ps[:, i * 128:(i + 1) * 128], lhsT=GG[:, t, d * 8:(d + 1) * 8],
                                                    rhs=cf[:, 4 + d, :], start=True, stop=True),
                     reads=[RGG, self.Rcf], writes=[Rps], pe_accum=True, sig=(i == n - 1))
            c.op("dve", lambda: nc.vector.tensor_copy(out=gcs[:, d, g4 * 128:(g4 + n) * 128], in_=ps[:, 0:n * 128]),
                 reads=[Rps], adds=[Rgcs])
    for d in range(2):
        c.dma("sp", self.gcT_d[d * 8:(d + 1) * 8, :], gcs[:, d, :], reads=[Rgcs])


def _p2b(self, l):
    c, nc, S = self.c, self.nc, self.S
    NT = S // 128
    cf = self.cf
    epsb = c.sb("epsb2", [128, 1], F32); Reps = Reg()
    c.op("pool", lambda: nc.gpsimd.memset(epsb[:], EPS), writes=[Reps])
    CW = c.sb("CW", [128, 12, 5], F32); RCW = Reg()
    with nc.allow_non_contiguous_dma(reason="tiny conv weight transpose"):
        for ci in range(12):
            c.dma("sp", CW[:, ci, :], self.conv_w[l, :, ci * 128:(ci + 1) * 128].rearrange("j p -> p j"), adds=[RCW])
    Us = Rot(c, "U", 2, [128, S + 4], F32)
    for (U, RU) in Us.items:
        c.op("pool", lambda: nc.gpsimd.memset(U[:, 0:2], 0.0), adds=[RU])
        c.op("pool", lambda: nc.gpsimd.memset(U[:, S + 2:S + 4], 0.0), adds=[RU])
    ACCs = Rot(c, "ACC", 2, [128, S], F32)
    SQ = c.sb("SQ", [128, S], F32); RSQ = Reg()
    NRMs = Rot(c, "NRM", 2, [128, S], F32)
    NB16s = Rot(c, "NB16", 2, [128, S], BF16)
    TKs = Rot(c, "TK", 2, [128, NT, 128], F32)
    pls = Rot(c, "pl", 2, [128, 512], F32, psum=True)
    pts = Rot(c, "pt", 2, [128, 4, 128], F32, psum=True)
    for ci in range(12):
        U, RU = Us.next()
        ACC, RACC = ACCs.next()
        NRM, RNRM = NRMs.next()
        NB16, RNB = NB16s.next()
        TK, RTK = TKs.next()
        RNRM.writers = []
        RTK.writers = []
        e = "dve"
        eng = nc.vector
        c.dma("sp", U[:, 2:S + 2], self.qkvT[ci * 128:(ci + 1) * 128, :], adds=[RU])
        c.op(e, lambda: eng.tensor_scalar(out=ACC[:], in0=U[:, 0:S], scalar1=CW[:, ci, 0:1], scalar2=None, op0=ALU.mult),
             reads=[RU, RCW], writes=[RACC])
        for j in range(1, 5):
            c.op(e, lambda: eng.scalar_tensor_tensor(out=ACC[:], in0=U[:, j:j + S], scalar=CW[:, ci, j:j + 1], in1=ACC[:],
                                                     op0=ALU.mult, op1=ALU.add),
                 reads=[RU, RCW, RACC], writes=[RACC])
        RU.writers = []
        c.op("act", lambda: nc.scalar.activation(out=ACC[:], in_=ACC[:], func=AF.Silu), reads=[RACC], writes=[RACC])
        if ci < 8:
            c.op("act", lambda: nc.scalar.activation(out=SQ[:], in_=ACC[:], func=AF.Square), reads=[RACC], writes=[RSQ])
            for b in range(0, S, 512):
                ps, Rps = pls.next()
                c.op("pe", lambda: nc.tensor.matmul(ps[:], lhsT=cf[:, 6, :], rhs=SQ[:, b:b + 512], start=True, stop=True),
                     reads=[RSQ, self.Rcf], writes=[Rps])
                c.op("act", lambda: nc.scalar.activation(out=NRM[:, b:b + 512], in_=ps[:], func=AF.Ln, bias=epsb[:, 0:1]),
                     reads=[Rps, Reps], adds=[RNRM])
            c.op("act", lambda: nc.scalar.activation(out=NRM[:], in_=NRM[:], func=AF.Exp, scale=-0.5), reads=[RNRM], writes=[RNRM])
            c.op("dve", lambda: nc.vector.tensor_tensor(out=NRM[:], in0=NRM[:], in1=ACC[:], op=ALU.mult),
                 reads=[RNRM, RACC], writes=[RNRM])
            src, Rsrc = NRM, RNRM
            c.op("act", lambda: nc.scalar.copy(out=NB16[:], in_=NRM[:]), reads=[RNRM], writes=[RNB])
            dstT = self.qT_d if ci < 4 else self.kT_d
            c.dma("sp", dstT[(ci % 4) * 128:(ci % 4 + 1) * 128, :], NB16[:], reads=[RNB])
        else:
            src, Rsrc = ACC, RACC
        if ci >= 4:
            for g4 in range(0, NT, 4):
                ps, Rps = pts.next()
                n = min(4, NT - g4)
                for i in range(n):
                    t = g4 + i
                    c.op("pe", lambda: nc.tensor.transpose(ps[:, i, :], src[:, t * 128:(t + 1) * 128], cf[:, 7, :]),
                         reads=[Rsrc, self.Rcf], writes=[Rps], pe_accum=True, sig=(i == n - 1))
                ev = "dve" if (g4 // 4) % 2 == 0 else "act"
                if ev == "dve":
                    c.op("dve", lambda: nc.vector.tensor_copy(out=TK[:, g4:g4 + n, :], in_=ps[:, 0:n, :]), reads=[Rps], adds=[RTK])
                else:
                    c.op("act", lambda: nc.scalar.copy(out=TK[:, g4:g4 + n, :], in_=ps[:, 0:n, :]), reads=[Rps], adds=[RTK])
            dst = self.ktok_d if ci < 8 else self.vtok_d
            c.dma("sp", dst.rearrange("(t p) c -> p t c", p=128)[:, :, (ci % 4) * 128:(ci % 4 + 1) * 128], TK[:], reads=[RTK])


def _p3(self, l):
    c, nc, S = self.c, self.nc, self.S
    NT = S // 128
    cf = self.cf
    H8 = [128, 8, 128]
    TS = c.sb("TS3", [128, NT, 80], F32); RTS = Reg()
    c.dma("sp", TS[:], self.ts_d.rearrange("(t p) c -> p t c", p=128), writes=[RTS])
    EGLB = c.sb("EGLB", [64, NT * 32], F32); REGL = Reg()
    c.dma("sp", EGLB[:], self.egl_d.partition_broadcast(64), writes=[REGL])
    epsb = c.sb("epsb3", [128, 1], F32); Reps = Reg()
    c.op("pool", lambda: nc.gpsimd.memset(epsb[:], EPS), writes=[Reps])
    Sst = [c.sb(f"Sst{d}", [64, 8, 64], F32) for d in range(2)]
    RS = [Reg(f"S{d}") for d in range(2)]
    for d in range(2):
        c.op("pool", lambda: nc.gpsimd.memset(Sst[d][:], 0.0), writes=[RS[d]])

    PB = Rot(c, "PB", 8, [128, 512], F32, psum=True)
    cfb = c.sb("cfb", [128, 13, 128], BF16); Rcfb = Reg()
    c.op("dve", lambda: nc.vector.tensor_copy(out=cfb[:, 0:12, :], in_=cf[:, 22:34, :]), reads=[self.Rcf], adds=[Rcfb])
    c.op("dve", lambda: nc.vector.tensor_copy(out=cfb[:, 12, :], in_=cf[:, 7, :]), reads=[self.Rcf], adds=[Rcfb])
    with c.scope():
        class W:
            pass
        ws = [[None, None], [None, None]]
        for d in range(2):
          for sl_ in range(2):
            w = W()
            tag = f"{d}{sl_}"
            mk = lambda n, sh, dt: (c.sb(f"{n}{tag}", sh, dt), Reg(f"{n}{tag}"))
            mk2 = lambda n, sh, dt: (c.sb(f"{n}{tag}", sh, dt), [Reg(f"{n}{tag}a"), Reg(f"{n}{tag}b")])
            w.kT, w.RkT = mk("kT", [64, 8, 128], BF16)
            w.qT, w.RqT = mk("qT", [64, 8, 128], BF16)
            w.ktok, w.Rktok = mk("ktok", [128, 8, 64], F32)
            w.vtok, w.Rvtok = mk("vtok", [128, 8, 64], F32)
            w.OS, w.ROS = w.ktok, w.Rktok
            w.OP, w.ROP = w.vtok, w.Rvtok
            w.GR, w.RGR = mk("GR", H8, F32)
            w.TA, w.RTA = w.GR, w.RGR
            w.NN = w.GR[0:64].rearrange("p h i -> p (h i)").rearrange("p (g e) -> p g e", e=64)
            w.RNN = w.RGR
            w.TL, w.RTL = mk("TL", H8, F32)
            w.MT = w.TL[0:64].rearrange("p h i -> p (h i)").rearrange("p (g e) -> p g e", e=64)
            w.RMT = w.RTL
            w.QT, w.RQT = mk("QT", [64, 8, 128], F32)
            w.AT, _ = mk("AT", H8, BF16)
            w.RAT = [Reg(f"AT{tag}a"), Reg(f"AT{tag}b")]
            w.L = [mk2("L0_", H8, BF16)]
            w.Y6b = w.L[0]
            w.U = [mk2("U0_", H8, BF16)]
            w.Y = [mk2("Y0_", H8, BF16)]
            w.Dm = mk2("Dm", H8, BF16)
            w.DTm = mk2("DTm", H8, BF16)
            w.Xm = mk2("Xm", H8, BF16)
            w.NU = [mk2(f"NU{i}_", H8, BF16) for i in range(2)]
            w.KH, w.RKH = mk("KH", [128, 8, 64], BF16)
            ws[d][sl_] = w

        def bc8(ap2, n):
            return ap2.unsqueeze(2).to_broadcast([ap2.shape[0], 8, n])

        def mask(mi):
            return cf[:, mi, :].unsqueeze(1).to_broadcast(H8)

        def ps_view(db, dtype=None):
            return db[:, :].rearrange("p (h i) -> p h i", i=128)


        def v4(bank, np_=128):
            return bank[0:np_, :].rearrange("p (h i) -> p h i", i=128)

        def v8(bank, np_=128):
            return bank[0:np_, :].rearrange("p (h e) -> p h e", e=64)

        def steps(d, t, slot):
            w = ws[d][slot]
            sl = slice(t * 128, (t + 1) * 128)
            c.dma("sp", w.kT[:], self.kT_d.rearrange("(h e) s -> e h s", e=64)[:, :, sl], writes=[w.RkT])
            c.dma("sp", w.qT[:], self.qT_d.rearrange("(h e) s -> e h s", e=64)[:, :, sl], writes=[w.RqT])
            c.dma("sp", w.ktok[:].rearrange("p h e -> p (h e)"), self.ktok_d[sl, :], writes=[w.Rktok])
            c.dma("sp", w.vtok[:].rearrange("p h e -> p (h e)"), self.vtok_d[sl, :], writes=[w.Rvtok])
            c.dma("sp", w.GR[:], self.gcT_d[d * 8:(d + 1) * 8, sl].partition_broadcast(128), writes=[w.RGR])
            for _ in range(3):
                yield
            gcb = TS[:, t, 0 + d * 8:8 + d * 8]
            gcbb = TS[:, t, 16 + d * 8:24 + d * 8]
            beta = TS[:, t, 32 + d * 8:40 + d * 8]
            beg = TS[:, t, 48 + d * 8:56 + d * 8]
            dk = TS[:, t, 64 + d * 8:72 + d * 8]
            c.op("act", lambda: nc.scalar.activation(out=w.TL[0:64], in_=w.GR[0:64], func=AF.Exp), reads=[w.RGR], writes=[w.RTL])
            c.op("dve", lambda: nc.vector.scalar_tensor_tensor(out=w.QT[:], in0=w.qT[:], scalar=0.125, in1=w.TL[0:64],
                                                               op0=ALU.mult, op1=ALU.mult),
                 reads=[w.RqT, w.RTL], writes=[w.RQT])
            c.op("pool", lambda: nc.gpsimd.tensor_tensor(out=w.TL[:], in0=bc8(gcbb, 128), in1=w.GR[:], op=ALU.subtract),
                 reads=[w.RGR, RTS], writes=[w.RTL])
            c.op("dve", lambda: nc.vector.tensor_tensor(out=w.TA[:], in0=w.GR[:], in1=bc8(gcb, 128), op=ALU.subtract),
                 reads=[w.RGR, RTS], writes=[w.RTA])
            c.op("dve", lambda: nc.vector.tensor_tensor(out=w.TL[:], in0=w.TL[:], in1=mask(1 if d == 0 else 3), op=ALU.add),
                 reads=[w.RTL, self.Rcf], writes=[w.RTL])
            c.op("dve", lambda: nc.vector.tensor_tensor(out=w.TA[:], in0=w.TA[:], in1=mask(0 if d == 0 else 2), op=ALU.add),
                 reads=[w.RTA, self.Rcf], writes=[w.RTA])
            c.op("act", lambda: nc.scalar.activation(out=w.TL[:], in_=w.TL[:], func=AF.Exp), reads=[w.RTL], writes=[w.RTL])
            c.op("act", lambda: nc.scalar.activation(out=w.TA[:], in_=w.TA[:], func=AF.Exp), reads=[w.RTA], writes=[w.RTA])
            Y0, RY0 = w.Y[0]
            for hf in range(2):
                hs = slice(hf * 4, hf * 4 + 4)
                c.op("pool", lambda: nc.gpsimd.tensor_tensor(out=Y0[:, hs, 0:64], in0=w.vtok[:, hs, :], in1=bc8(beta, 64)[:, hs, :], op=ALU.mult),
                     reads=[w.Rvtok, RTS], adds=[RY0[hf]])
                c.op("pool", lambda: nc.gpsimd.tensor_tensor(out=Y0[:, hs, 64:128], in0=w.ktok[:, hs, :], in1=bc8(beg, 64)[:, hs, :], op=ALU.mult),
                     reads=[w.Rktok, RTS], adds=[RY0[hf]])
            c.op("pool", lambda: nc.gpsimd.tensor_tensor(out=w.KH[:], in0=w.ktok[:], in1=bc8(dk, 64), op=ALU.mult),
                 reads=[w.Rktok, RTS], writes=[w.RKH])
            for _ in range(5):
                yield
            L0, RL0 = w.L[0]
            for hf in range(2):
                hs = slice(hf * 4, hf * 4 + 4)
                bk, Rbk = PB.next()
                for h in range(hf * 4, hf * 4 + 4):
                    c.op("pe", lambda: nc.tensor.matmul(v4(bk)[:, h % 4, :], lhsT=w.kT[:, h, :], rhs=w.kT[:, h, :], start=True, stop=True),
                         reads=[w.RkT], writes=[Rbk], pe_accum=True, sig=(h % 4 == 3))
                c.op("dve", lambda: nc.vector.tensor_tensor(out=L0[:, hs, :], in0=v4(bk), in1=w.TL[:, hs, :], op=ALU.mult),
                     reads=[Rbk, w.RTL], writes=[RL0[hf]])
                bq, Rbq = PB.next()
                for h in range(hf * 4, hf * 4 + 4):
                    c.op("pe", lambda: nc.tensor.matmul(v4(bq)[:, h % 4, :], lhsT=w.kT[:, h, :], rhs=w.qT[:, h, :], start=True, stop=True),
                         reads=[w.RkT, w.RqT], writes=[Rbq], pe_accum=True, sig=(h % 4 == 3))
                c.op("dve", lambda: nc.vector.scalar_tensor_tensor(out=w.AT[:, hs, :], in0=v4(bq), scalar=0.125, in1=w.TA[:, hs, :],
                                                                   op0=ALU.mult, op1=ALU.mult),
                     reads=[Rbq, w.RTA], writes=[w.RAT[hf]])
                yield
            if d == 0 and t == 0:
                self.dump("AT", w.AT[:], w.RAT[1], BF16); self.dump("L0", L0[:], RL0[1], BF16)
            U0, RU0 = w.U[0]
            for hf in range(2):
                hs = slice(hf * 4, hf * 4 + 4)
                bk, Rbk = PB.next()
                puv = bk[:, :].bitcast(BF16)[:, 0:512].rearrange("p (h i) -> p h i", i=128)
                for h in range(hf * 4, hf * 4 + 4):
                    c.op("pe", lambda: nc.tensor.transpose(puv[:, h % 4, :], L0[:, h, :], self.identb[:]),
                         reads=[RL0[hf], self.Ridb], writes=[Rbk], pe_accum=True, sig=(h % 4 == 3))
                c.op("act", lambda: nc.scalar.copy(out=U0[:, hs, :], in_=puv), reads=[Rbk], writes=[RU0[hf]])
            yield
            Dm, RDm = w.Dm
            DTm, RDTm = w.DTm
            Xm, RXm = w.Xm
            Yc, RYc = w.Y[0]
            nL = (lambda k: 22 + k) if d == 0 else (lambda k: 28 + k)
            nU = (lambda k: 28 + k) if d == 0 else (lambda k: 22 + k)
            m4 = lambda mi: cf[:, mi, :].unsqueeze(1).to_broadcast([128, 4, 128])
            m4b = lambda mi: cfb[:, mi - 22, :].unsqueeze(1).to_broadcast([128, 4, 128])
            i4b = cfb[:, 12, :].unsqueeze(1).to_broadcast([128, 4, 128])
            for hf in range(2):
                hs = slice(hf * 4, hf * 4 + 4)
                c.op("dve", lambda: nc.vector.tensor_tensor(out=Dm[:, hs, :], in0=L0[:, hs, :], in1=m4b(nL(0)), op=ALU.mult),
                     reads=[RL0[hf], Rcfb], writes=[RDm[hf]])
                c.op("dve", lambda: nc.vector.tensor_tensor(out=Dm[:, hs, :], in0=Dm[:, hs, :], in1=i4b, op=ALU.add),
                     reads=[RDm[hf], Rcfb], writes=[RDm[hf]])
                c.op("dve", lambda: nc.vector.tensor_tensor(out=DTm[:, hs, :], in0=U0[:, hs, :], in1=m4b(nU(0)), op=ALU.mult),
                     reads=[RU0[hf], Rcfb], writes=[RDTm[hf]])
                c.op("dve", lambda: nc.vector.tensor_tensor(out=DTm[:, hs, :], in0=DTm[:, hs, :], in1=i4b, op=ALU.add),
                     reads=[RDTm[hf], Rcfb], writes=[RDTm[hf]])
            yield
            for k in range(1, 6):
                NU, RNU = w.NU[k % 2]
                for hf in range(2):
                    hs = slice(hf * 4, hf * 4 + 4)
                    hh = range(hf * 4, hf * 4 + 4)
                    c.op("dve", lambda: nc.vector.tensor_tensor(out=NU[:, hs, :], in0=U0[:, hs, :], in1=m4b(nU(k)), op=ALU.mult),
                         reads=[RU0[hf], Rcfb], writes=[RNU[hf]])
                    b1, Rb1 = PB.next()
                    for h in hh:
                        c.op("pe", lambda: nc.tensor.matmul(v4(b1)[:, h % 4, :], lhsT=NU[:, h, :], rhs=Dm[:, h, :], start=True, stop=True),
                             reads=[RNU[hf], RDm[hf]], writes=[Rb1], pe_accum=True, sig=(h % 4 == 3))
                    c.op("dve", lambda: nc.vector.tensor_tensor(out=Xm[:, hs, :], in0=v4(b1), in1=m4(7), op=ALU.add),
                         reads=[Rb1, self.Rcf], writes=[RXm[hf]])
                    b3, Rb3 = PB.next()
                    for h in hh:
                        c.op("pe", lambda: nc.tensor.matmul(v4(b3)[:, h % 4, :], lhsT=DTm[:, h, :], rhs=Xm[:, h, :], start=True, stop=True),
                             reads=[RDTm[hf], RXm[hf]], writes=[Rb3], pe_accum=True, sig=(h % 4 == 3))
                    b4, Rb4 = PB.next()
                    for h in hh:
                        c.op("pe", lambda: nc.tensor.matmul(v4(b4)[:, h % 4, :], lhsT=Xm[:, h, :], rhs=DTm[:, h, :], start=True, stop=True),
                             reads=[RDTm[hf], RXm[hf]], writes=[Rb4], pe_accum=True, sig=(h % 4 == 3))
                    c.op("act", lambda: nc.scalar.copy(out=Dm[:, hs, :], in_=v4(b3)), reads=[Rb3], writes=[RDm[hf]])
                    c.op("act", lambda: nc.scalar.copy(out=DTm[:, hs, :], in_=v4(b4)), reads=[Rb4], writes=[RDTm[hf]])
                    yield
            Y6, RY6 = w.Y6b
            for hf in range(2):
                hs = slice(hf * 4, hf * 4 + 4)
                bk, Rbk = PB.next()
                for h in range(hf * 4, hf * 4 + 4):
                    c.op("pe", lambda: nc.tensor.matmul(v4(bk)[:, h % 4, :], lhsT=DTm[:, h, :], rhs=Yc[:, h, :], start=True, stop=True),
                         reads=[RDTm[hf], RYc[hf]], writes=[Rbk], pe_accum=True, sig=(h % 4 == 3))
                c.op("act", lambda: nc.scalar.copy(out=Y6[:, hs, :], in_=v4(bk)), reads=[Rbk], writes=[RY6[hf]])
            yield
            if d == 0 and t == 0:
                self.dump("Y6", Y6[:], RY6[1], BF16)
            bk, Rbk = PB.next()
            for h in range(8):
                c.op("pe", lambda: nc.tensor.matmul(v8(bk)[:, h, :], lhsT=w.AT[:, h, :], rhs=Y6[:, h, 0:64], start=True, stop=True),
                     reads=[w.RAT[h // 4], RY6[h // 4]], writes=[Rbk], pe_accum=True, sig=(h == 7))
            c.op("act", lambda: nc.scalar.copy(out=w.OP[:], in_=v8(bk)), reads=[Rbk], writes=[w.ROP])
            yield
            for hf in range(2):
                hs = slice(hf * 4, hf * 4 + 4)
                bk, Rbk = PB.next()
                for h in range(hf * 4, hf * 4 + 4):
                    c.op("pe", lambda: nc.tensor.matmul(v4(bk, 64)[:, h % 4, :], lhsT=Y6[:, h, 64:128], rhs=w.AT[:, h, :], start=True, stop=True),
                         reads=[w.RAT[hf], RY6[hf]], writes=[Rbk], pe_accum=True, sig=(h % 4 == 3))
                c.op("dve", lambda: nc.vector.tensor_tensor(out=w.QT[:, hs, :], in0=w.QT[:, hs, :], in1=v4(bk, 64), op=ALU.subtract),
                     reads=[w.RQT, Rbk], writes=[w.RQT])
            yield
            eg = EGLB[:, (t * 2 + d) * 16:(t * 2 + d + 1) * 16]
            c.op("pool", lambda: nc.gpsimd.tensor_tensor(out=w.MT[:], in0=cf[0:64, 7, 0:64].unsqueeze(1).to_broadcast([64, 16, 64]),
                                                         in1=eg.unsqueeze(2).to_broadcast([64, 16, 64]), op=ALU.mult),
                 reads=[self.Rcf, REGL], writes=[w.RMT])
            for cc in range(2):
                rows = slice(cc * 64, (cc + 1) * 64)
                gs = slice(cc * 8, cc * 8 + 8)
                bk, Rbk = PB.next()
                for h in range(8):
                    c.op("pe", lambda: nc.tensor.matmul(v8(bk, 64)[:, h, :], lhsT=Y6[rows, h, 64:128], rhs=w.KH[rows, h, :],
                                                        start=True, stop=True),
                         reads=RY6 + [w.RKH], writes=[Rbk], pe_accum=True, sig=(h == 7))
                c.op("dve", lambda: nc.vector.tensor_tensor(out=w.MT[:, gs, :], in0=w.MT[:, gs, :], in1=v8(bk, 64), op=ALU.subtract),
                     reads=[w.RMT, Rbk], writes=[w.RMT])
                bk, Rbk = PB.next()
                for h in range(8):
                    c.op("pe", lambda: nc.tensor.matmul(v8(bk, 64)[:, h, :], lhsT=w.KH[rows, h, :], rhs=Y6[rows, h, 0:64],
                                                        start=True, stop=True),
                         reads=RY6 + [w.RKH], writes=[Rbk], pe_accum=True, sig=(h == 7))
                c.op("act", lambda: nc.scalar.copy(out=w.NN[:, gs, :], in_=v8(bk, 64)), reads=[Rbk], writes=[w.RNN])
                yield
            if d == 0 and t == 0:
                self.dump("OP", w.OP[:], w.ROP); self.dump("QT", w.QT[:], w.RQT)
                self.dump("MT", w.MT[:], w.RMT); self.dump("NN", w.NN[:], w.RNN)
            for cc in ((0, 1) if d == 0 else (1, 0)):
                rows = slice(cc * 64, (cc + 1) * 64)
                bs, Rbs = PB.next()
                for h in range(8):
                    c.op("pe", lambda: nc.tensor.matmul(v8(bs, 64)[:, h, :], lhsT=w.MT[:, cc * 8 + h, :], rhs=Sst[d][:, h, :], start=True, stop=True),
                         reads=[w.RMT, RS[d]], writes=[Rbs], pe_accum=True, sig=(h == 7))
                bo, Rbo = PB.next()
                for h in range(8):
                    c.op("pe", lambda: nc.tensor.matmul(v8(bo)[rows, h, :], lhsT=w.QT[:, h, rows], rhs=Sst[d][:, h, :], start=True, stop=True),
                         reads=[w.RQT, RS[d]], writes=[Rbo], pe_accum=True, sig=(h == 7))
                c.op("dve", lambda: nc.vector.tensor_tensor(out=Sst[d][:], in0=v8(bs, 64), in1=w.NN[:, cc * 8:(cc + 1) * 8, :], op=ALU.add),
                     reads=[Rbs, w.RNN], writes=[RS[d]])
                c.op("dve", lambda: nc.vector.tensor_tensor(out=w.OS[rows], in0=v8(bo)[rows], in1=w.OP[rows], op=ALU.add),
                     reads=[Rbo, w.ROP], writes=[w.ROS])
                yield
            c.dma("sp", self.odir_d[d][sl, :], w.OS[:].rearrange("p h e -> p (h e)"), reads=[w.ROS])

        HALF = 15; DEPH = 0
        seqs = [list(range(NT)), list(range(NT - 1, -1, -1))]
        nxt = [0, 0]
        active = [[], []]
        tick = 0
        while any(nxt[d] < NT or active[d] for d in range(2)):
            tick += 1
            for d in range(2):
                if d == 1 and tick <= DEPH:
                    continue
                if nxt[d] < NT and len(active[d]) < 2 and (not active[d] or active[d][-1][1] >= HALF):
                    active[d].append([steps(d, seqs[d][nxt[d]], nxt[d] % 2), 0])
                    nxt[d] += 1
            for pos in range(2):
                for d in range(2):
                    if pos < len(active[d]):
                        ent = active[d][pos]
                        try:
                            next(ent[0])
                            ent[1] += 1
                        except StopIteration:
                            ent[1] = -1
            for d in range(2):
                active[d] = [e for e in active[d] if e[1] >= 0]
    c.drain_dmas("sp")
    NW = c.sb("NW", [128, 64], F32); RNW = Reg()
    c.dma("sp", NW[:], self.dn_norm_w[l:l + 1, :].partition_broadcast(128), writes=[RNW])
    zts = Rot(c, "zt", 2, [128, 8, 64], F32)
    ofs = Rot(c, "of", 2, [128, 8, 64], F32)
    obs = Rot(c, "ob", 2, [128, 8, 64], F32)
    sqs = Rot(c, "sq", 2, [128, 8, 64], F32)
    sst = Rot(c, "sst", 2, [128, 16], F32)
    oas = Rot(c, "oa", 2, [128, 512], BF16)
    oTs = Rot(c, "oT", 2, [128, 4, 128], BF16)
    for t in range(NT):
        zt, Rzt = zts.next()
        sq, Rsq = sqs.next()
        st, Rst = sst.next()
        oa, Roa = oas.next()
        oT, RoT = oTs.next()
        of_, Rof = ofs.next()
        ob_, Rob = obs.next()
        c.dma("sp", of_[:].rearrange("p h e -> p (h e)"), self.odir_d[0][t * 128:(t + 1) * 128, :], writes=[Rof])
        c.dma("sp", ob_[:].rearrange("p h e -> p (h e)"), self.odir_d[1][t * 128:(t + 1) * 128, :], writes=[Rob])
        c.op("pool", lambda: nc.gpsimd.tensor_tensor(out=of_[:], in0=of_[:], in1=ob_[:], op=ALU.add), reads=[Rof, Rob], writes=[Rof])
        ot = of_[:]
        c.dma("sp", zt[:].rearrange("p h e -> p (h e)"), self.ztok[t * 128:(t + 1) * 128, :], writes=[Rzt])
        c.op("act", lambda: nc.scalar.activation(out=sq[:], in_=ot, func=AF.Square), reads=[Rof], writes=[Rsq])
        c.op("dve", lambda: nc.vector.tensor_reduce(out=st[:, 0:8], in_=sq[:], axis=AX.X, op=ALU.add), reads=[Rsq], writes=[Rst])
        c.op("act", lambda: nc.scalar.activation(out=st[:, 8:16], in_=st[:, 0:8], func=AF.Sqrt, bias=epsb[:, 0:1], scale=1.0 / 64),
             reads=[Rst, Reps], writes=[Rst])
        c.op("dve", lambda: nc.vector.reciprocal(out=st[:, 8:16], in_=st[:, 8:16]), reads=[Rst], writes=[Rst])
        c.op("dve", lambda: nc.vector.tensor_tensor(out=sq[:], in0=ot, in1=bc8(st[:, 8:16], 64), op=ALU.mult),
             reads=[Rof, Rst], writes=[Rsq])
        c.op("pool", lambda: nc.gpsimd.tensor_tensor(out=sq[:], in0=sq[:], in1=NW[:, :].unsqueeze(1).to_broadcast([128, 8, 64]), op=ALU.mult),
             reads=[Rsq, RNW], writes=[Rsq])
        c.op("act", lambda: nc.scalar.activation(out=zt[:], in_=zt[:], func=AF.Silu), reads=[Rzt], writes=[Rzt])
        c.op("dve", lambda: nc.vector.tensor_tensor(out=oa[:].rearrange("p (h e) -> p h e", e=64), in0=sq[:], in1=zt[:], op=ALU.mult),
             reads=[Rsq, Rzt], writes=[Roa])
        PT, RPT = PB.next()
        ptv = PT[:, :].bitcast(BF16)[:, 0:512].rearrange("p (k i) -> p k i", i=128)
        for k in range(4):
            c.op("pe", lambda: nc.tensor.transpose(ptv[:, k, :], oa[:, k * 128:(k + 1) * 128], self.identb[:]),
                 reads=[Roa, self.Ridb], writes=[RPT], pe_accum=True, sig=(k == 3))
        c.op("act", lambda: nc.scalar.copy(out=oT[:], in_=ptv), reads=[RPT], writes=[RoT])
        c.dma("sp", self.oaT_d.rearrange("(k p) s -> p k s", p=128)[:, :, t * 128:(t + 1) * 128], oT[:], reads=[RoT])


def _p4(self, l):
    c, nc, S = self.c, self.nc, self.S
    NT = S // 128
    AQ = c.sb("AQ", [128, 8, S], BF16); RAQ = Reg()
    AK = c.sb("AK", [128, 2, S], BF16); RAK = Reg()
    c.op("pool", lambda: nc.gpsimd.memset(AQ[64:128], 0.0), adds=[RAQ])
    c.op("pool", lambda: nc.gpsimd.memset(AK[64:128], 0.0), adds=[RAK])
    AV = c.sb("AV", [128, NT, 128], BF16); RAV = Reg()
    CB = c.sb("CB", [128, 3, 8, 128], BF16); RCB = Reg()
    c.dma("sp", AQ[0:64], self.aqT.rearrange("(h e) s -> e h s", e=64), adds=[RAQ])
    c.dma("sp", AK[0:64], self.akT.rearrange("(h e) s -> e h s", e=64), adds=[RAK])
    c.dma("sp", AV[:], self.avtok.rearrange("(t p) c -> p t c", p=128), writes=[RAV])
    c.dma("sp", CB[:], self.cb_d[:, :, :, :], writes=[RCB])
    ones = c.sb("ones", [128, 64], BF16); Rones = Reg()
    c.op("pool", lambda: nc.gpsimd.memset(ones[:], 1.0), writes=[Rones])
    ES = c.sb("ES", [64, 8], F32); RES = Reg()
    c.dma("sp", ES[:], self.attn_sink[l:l + 1, :].partition_broadcast(64), writes=[RES])
    c.op("act", lambda: nc.scalar.activation(out=ES[:], in_=ES[:], func=AF.Exp), reads=[RES], writes=[RES])
    OBT = c.sb("OBT", [128, 4, S], BF16); ROBT = Reg()
    psc = Rot(c, "psc", 3, [128, 512], F32, psum=True)
    ppv = Rot(c, "ppv", 2, [128, 512], F32, psum=True)
    pdn = Rot(c, "pdn", 2, [128, 512], F32, psum=True)
    PTs = Rot(c, "PT", 3, [128, 4, 128], BF16)
    dens = Rot(c, "den", 2, [64, 4, 128], F32)
    for n in range(NT):
        qs = slice(n * 128, (n + 1) * 128)
        for g in range(2):
            kts = [kt for kt in (n - 1, n, n + 1) if 0 <= kt < NT]
            pv, Rpv = ppv.next()
            dn, Rdn = pdn.next()
            pvv = pv[0:64, :].rearrange("p (h i) -> p h i", i=128)
            dnv = dn[0:64, :].rearrange("p (h i) -> p h i", i=128)
            for ki, kt in enumerate(kts):
                sc, Rsc = psc.next()
                scv = sc[:, :].rearrange("p (h i) -> p h i", i=128)
                c.op("pe", lambda: nc.tensor.matmul(scv, lhsT=AK[:, g, kt * 128:(kt + 1) * 128], rhs=AQ[:, g * 4:(g + 1) * 4, qs],
                                                    start=True, stop=False),
                     reads=[RAK, RAQ], writes=[Rsc], pe_accum=True, sig=False)
                c.op("pe", lambda: nc.tensor.matmul(scv, lhsT=self.identb[:], rhs=CB[:, kt - n + 1, g * 4:(g + 1) * 4, :],
                                                    start=False, stop=True),
                     reads=[RCB, self.Ridb], writes=[Rsc], pe_accum=True)
                PT, RPT = PTs.next()
                c.op("act", lambda: nc.scalar.activation(out=PT[:], in_=scv, func=AF.Exp, scale=0.125), reads=[Rsc], writes=[RPT])
                c.op("pe", lambda: nc.tensor.matmul(pvv, lhsT=AV[:, kt, g * 64:(g + 1) * 64], rhs=PT[:],
                                                    start=(ki == 0), stop=(ki == len(kts) - 1)),
                     reads=[RAV, RPT], writes=[Rpv], pe_accum=True, sig=(ki == len(kts) - 1))
                c.op("pe", lambda: nc.tensor.matmul(dnv, lhsT=ones[:], rhs=PT[:],
                                                    start=(ki == 0), stop=(ki == len(kts) - 1)),
                     reads=[Rones, RPT], writes=[Rdn], pe_accum=True, sig=(ki == len(kts) - 1))
            den, Rden = dens.next()
            c.op("dve", lambda: nc.vector.tensor_tensor(out=den[:], in0=dnv, in1=ES[:, g * 4:(g + 1) * 4].unsqueeze(2).to_broadcast([64, 4, 128]),
                                                        op=ALU.add),
                 reads=[Rdn, RES], writes=[Rden])
            c.op("dve", lambda: nc.vector.reciprocal(out=den[:], in_=den[:]), reads=[Rden], writes=[Rden])
            pv2 = pv[0:64, :].rearrange("p (a b i) -> p a b i", b=2, i=128)
            den2 = den[:].rearrange("p (a b) i -> p a b i", b=2)
            for par in range(2):
                c.op("dve", lambda: nc.vector.tensor_tensor(out=OBT[par * 64:(par + 1) * 64, g * 2:(g + 1) * 2, qs],
                                                            in0=pv2[:, :, par, :], in1=den2[:, :, par, :], op=ALU.mult),
                     reads=[Rpv, Rden], adds=[ROBT])
    c.dma("sp", self.obT_d.rearrange("(k p) s -> p k s", p=128), OBT[:], reads=[ROBT])


def _resid_epilogue(self, banks, xt, Rxt, GP, RGP, ss, Rss, k, dst, outs):
    c, nc = self.c, self.nc
    junk, Rj = self.junk
    for hf, (bk, Rbk) in enumerate(banks):
        c.op("act", lambda: nc.scalar.activation(out=junk[:, 0:512], in_=bk[:], func=AF.Square, accum_out=ss[:, 4 * k + hf:4 * k + hf + 1]),
             reads=[Rbk], writes=[Rj, Rss])
    c.op("dve", lambda: nc.vector.tensor_tensor(out=ss[:, 4 * k + 2:4 * k + 3], in0=ss[:, 4 * k:4 * k + 1], in1=ss[:, 4 * k + 1:4 * k + 2], op=ALU.add),
         reads=[Rss], writes=[Rss])
    self.rstd_of(ss[:, 4 * k + 2:4 * k + 3], ss[:, 4 * k + 3:4 * k + 4], Rss, D)
    xo, Rxo = outs.next()
    for hf, (bk, Rbk) in enumerate(banks):
        hs = slice(hf * 512, (hf + 1) * 512)
        c.op("dve", lambda: nc.vector.scalar_tensor_tensor(out=xo[:, hs], in0=bk[:], scalar=ss[:, 4 * k + 3:4 * k + 4], in1=GP[:, hs],
                                                           op0=ALU.mult, op1=ALU.mult),
             reads=[Rbk, Rss, RGP], adds=[Rxo])
    c.op("pool", lambda: nc.gpsimd.tensor_tensor(out=xo[:], in0=xo[:], in1=xt[:], op=ALU.add), reads=[Rxo, Rxt], writes=[Rxo])
    c.dma("sp", dst, xo[:], reads=[Rxo])
    Rxo.writers = [w for w in Rxo.writers[-1:]]


def _p5(self, l, Xin, Xout):
    c, nc, S = self.c, self.nc, self.S
    NB = S // 512
    self.epsb = c.sb("epsb5", [128, 1], F32); self.Reps = Reg()
    c.op("pool", lambda: nc.gpsimd.memset(self.epsb[:], EPS), writes=[self.Reps])
    WA = c.sb("WA", [128, 4, D], BF16); WB = c.sb("WB", [128, 4, D], BF16); WO = c.sb("WO", [128, 8, D], BF16)
    RWA = [[] for _ in range(4)]; RWB = [[] for _ in range(4)]; RWO = [[] for _ in range(8)]
    for kc in range(4):
        self.load_wcast(WA[:, kc, :], self.w_up_a[l, kc * 128:(kc + 1) * 128, :], RWA[kc], D)
        self.load_wcast(WB[:, kc, :], self.w_up_b[l, kc * 128:(kc + 1) * 128, :], RWB[kc], D)
    for kc in range(8):
        self.load_wcast(WO[:, kc, :], self.w_out[l, kc * 128:(kc + 1) * 128, :], RWO[kc], D)
    GP = c.sb("GP", [128, D], F32); RGP = Reg()
    c.dma("sp", GP[:], self.n_mix_post[l:l + 1, :].partition_broadcast(128), writes=[RGP])
    self.junk = (c.sb("junk5", [128, 512], BF16), Reg())
    ss = c.sb("ss5", [128, 4 * NB * 4], F32); Rss = Reg()
    c.op("pool", lambda: nc.gpsimd.memset(ss[:], 0.0), writes=[Rss])
    OAs = Rot(c, "OA", 2, [128, 4, 512], BF16); OBs = Rot(c, "OB", 2, [128, 4, 512], BF16)
    SGs = Rot(c, "SG", 2, [128, 16, 512], F32)
    MXs = Rot(c, "MX", 2, [128, 8, 512], BF16)
    t1s = Rot(c, "t1", 2, [128, 512], F32); t2s = Rot(c, "t2", 2, [128, 512], F32)
    xts = Rot(c, "xt5", 2, [128, D], F32); outs = Rot(c, "xo5", 2, [128, D], F32)
    pbk = Rot(c, "pb5", 8, [128, 512], F32, psum=True)
    for b in range(NB):
        bs = slice(b * 512, (b + 1) * 512)
        OA, ROA_ = OAs.next(); OB, ROB_ = OBs.next(); SG, RSG = SGs.next(); MX, RMX = MXs.next()
        c.dma("sp", OA[:], self.oaT_d.rearrange("(k p) s -> p k s", p=128)[:, :, bs], writes=[ROA_])
        c.dma("sp", OB[:], self.obT_d.rearrange("(k p) s -> p k s", p=128)[:, :, bs], writes=[ROB_])
        c.dma("sp", SG[:], self.sgT.rearrange("(k p) s -> p k s", p=128)[:, :, bs], writes=[RSG])
        RMX.writers = []
        for dc in range(8):
            ds_ = slice(dc * 128, (dc + 1) * 128)
            pa, Rpa = pbk.next()
            for kc in range(4):
                c.op("pe", lambda: nc.tensor.matmul(pa[:], lhsT=WA[:, kc, ds_], rhs=OA[:, kc, :], start=(kc == 0), stop=(kc == 3)),
                     reads=RWA[kc] + [ROA_], writes=[Rpa], pe_accum=True, sig=(kc == 3))
            t1, Rt1 = t1s.next()
            c.op("dve", lambda: nc.vector.tensor_tensor(out=t1[:], in0=pa[:], in1=SG[:, dc, :], op=ALU.mult), reads=[Rpa, RSG], writes=[Rt1])
            pb, Rpb = pbk.next()
            for kc in range(4):
                c.op("pe", lambda: nc.tensor.matmul(pb[:], lhsT=WB[:, kc, ds_], rhs=OB[:, kc, :], start=(kc == 0), stop=(kc == 3)),
                     reads=RWB[kc] + [ROB_], writes=[Rpb], pe_accum=True, sig=(kc == 3))
            t2, Rt2 = t2s.next()
            c.op("dve", lambda: nc.vector.tensor_tensor(out=t2[:], in0=pb[:], in1=SG[:, 8 + dc, :], op=ALU.mult), reads=[Rpb, RSG], writes=[Rt2])
            c.op("pool", lambda: nc.gpsimd.tensor_tensor(out=MX[:, dc, :], in0=t1[:], in1=t2[:], op=ALU.add), reads=[Rt1, Rt2], adds=[RMX])
        for tt in range(4):
            k = b * 4 + tt
            r0 = b * 512 + tt * 128
            xt, Rxt = xts.next()
            c.dma("sp", xt[:], Xin[r0:r0 + 128, :], writes=[Rxt])
            banks = []
            for hf in range(2):
                bk, Rbk = pbk.next()
                for kc in range(8):
                    c.op("pe", lambda: nc.tensor.matmul(bk[:], lhsT=MX[:, kc, tt * 128:(tt + 1) * 128], rhs=WO[:, kc, hf * 512:(hf + 1) * 512],
                                                        start=(kc == 0), stop=(kc == 7)),
                         reads=RWO[kc] + [RMX], writes=[Rbk], pe_accum=True, sig=(kc == 7))
                banks.append((bk, Rbk))
            self.resid_epilogue(banks, xt, Rxt, GP, RGP, ss, Rss, k, Xout[r0:r0 + 128, :], outs)


def _p6a(self, l, X):
    c, nc, S = self.c, self.nc, self.S
    NB = S // 512
    self.epsb = c.sb("epsb6", [128, 1], F32); self.Reps = Reg()
    c.op("pool", lambda: nc.gpsimd.memset(self.epsb[:], EPS), writes=[self.Reps])
    W1 = c.sb("W1", [128, 8, DFF], BF16)
    RW1 = [[] for _ in range(8)]
    for c0 in range(0, DFF, 1024):
        for kc in range(8):
            r = Reg("w")
            c.dma("pool", W1[:, kc, c0:c0 + 1024], self.w1[l, kc * 128:(kc + 1) * 128, c0:c0 + 1024], writes=[r])
            RW1[kc].append(r)
    gammaT = c.sb("gammaT6", [128, 8], F32); self.Rgam = Reg()
    with nc.allow_non_contiguous_dma(reason="tiny gamma transpose"):
        c.dma("sp", gammaT[:], self.n_mlp_pre.rearrange("l (c p) -> l p c", p=128)[l], writes=[self.Rgam])
    xts = Rot(c, "xt6", 2, [128, D], F32)
    self.junk = (c.sb("junk6", [128, D], BF16), Reg())
    self.xn = (c.sb("xn6", [128, D], BF16), Reg())
    ss = c.sb("ss6", [128, 2 * NB * 4], F32); Rss = Reg()
    c.op("pool", lambda: nc.gpsimd.memset(ss[:], 0.0), writes=[Rss])
    pT = c.ps("pT6", [128, 8, 128], BF16); RpT = Reg("pT6", excl=True)
    hTs = Rot(c, "hT6", 2, [128, 8, 512], BF16)
    pss = Rot(c, "pf6", 6, [128, 512], F32, psum=True)
    rls = Rot(c, "rl", 3, [128, 512], F32)
    ubs = Rot(c, "ub", 3, [128, 512], BF16)
    hbuf = [hTs.next() for _ in range(2)]

    def emit_norm(bb, tt):
        hTn, RhTn = hbuf[bb % 2]
        if tt == 0:
            RhTn.writers = []
        self.norm_to_hT(X, bb * 512 + tt * 128, l, gammaT, hTn, RhTn, tt, xts, ss, Rss, bb * 4 + tt, pT, RpT)

    for tt in range(4):
        emit_norm(0, tt)
    for b in range(NB):
        hT, RhT = hbuf[b % 2]
        for fc in range(DFF // 128):
            if b + 1 < NB and fc in (6, 12, 18, 24):
                emit_norm(b + 1, (fc - 6) // 6)
            ps, Rps = pss.next()
            for kc in range(8):
                c.op("pe", lambda: nc.tensor.matmul(ps[:], lhsT=W1[:, kc, fc * 128:(fc + 1) * 128], rhs=hT[:, kc, :],
                                                    start=(kc == 0), stop=(kc == 7)),
                     reads=RW1[kc] + [RhT], writes=[Rps], pe_accum=True, sig=(kc == 7))
            rl, Rrl = rls.next()
            ub, Rub = ubs.next()
            c.op("act", lambda: nc.scalar.activation(out=rl[:], in_=ps[:], func=AF.Relu), reads=[Rps], writes=[Rrl])
            e = "dve" if fc % 2 == 0 else "pool"
            eng = nc.vector if e == "dve" else nc.gpsimd
            c.op(e, lambda: eng.tensor_tensor(out=ub[:], in0=rl[:], in1=rl[:], op=ALU.mult), reads=[Rrl], writes=[Rub])
            c.dma("sp", self.uT_d[fc * 128:(fc + 1) * 128, b * 512:(b + 1) * 512], ub[:], reads=[Rub])


def _p6b(self, l, Xin, Xout):
    c, nc, S = self.c, self.nc, self.S
    NB = S // 512
    NF = DFF // 128
    self.epsb = c.sb("epsb7", [128, 1], F32); self.Reps = Reg()
    c.op("pool", lambda: nc.gpsimd.memset(self.epsb[:], EPS), writes=[self.Reps])
    W2 = c.sb("W2", [128, NF, D], BF16)
    RW2 = [[] for _ in range(NF)]
    for fc in range(NF):
        self.load_wcast(W2[:, fc, :], self.w2[l, fc * 128:(fc + 1) * 128, :], RW2[fc], D)
    GP = c.sb("GP7", [128, D], F32); RGP = Reg()
    c.dma("sp", GP[:], self.n_mlp_post[l:l + 1, :].partition_broadcast(128), writes=[RGP])
    self.junk = (c.sb("junk7", [128, 512], BF16), Reg())
    ss = c.sb("ss7", [128, 4 * NB * 4], F32); Rss = Reg()
    c.op("pool", lambda: nc.gpsimd.memset(ss[:], 0.0), writes=[Rss])
    UTs = Rot(c, "UT", 2, [128, NF, 512], BF16)
    xts = Rot(c, "xt7", 2, [128, D], F32); outs = Rot(c, "xo7", 2, [128, D], F32)
    pbk = Rot(c, "pb7", 6, [128, 512], F32, psum=True)
    for b in range(NB):
        UT, RUT = UTs.next()
        c.dma("sp", UT[:], self.uT_d.rearrange("(k p) s -> p k s", p=128)[:, :, b * 512:(b + 1) * 512], writes=[RUT])
        for tt in range(4):
            k = b * 4 + tt
            r0 = b * 512 + tt * 128
            xt, Rxt = xts.next()
            c.dma("sp", xt[:], Xin[r0:r0 + 128, :], writes=[Rxt])
            banks = []
            for hf in range(2):
                bk, Rbk = pbk.next()
                for fc in range(NF):
                    c.op("pe", lambda: nc.tensor.matmul(bk[:], lhsT=UT[:, fc, tt * 128:(tt + 1) * 128], rhs=W2[:, fc, hf * 512:(hf + 1) * 512],
                                                        start=(fc == 0), stop=(fc == NF - 1)),
                         reads=RW2[fc] + [RUT], writes=[Rbk], pe_accum=True, sig=(fc == NF - 1))
                banks.append((bk, Rbk))
            self.resid_epilogue(banks, xt, Rxt, GP, RGP, ss, Rss, k, Xout[r0:r0 + 128, :], outs)


Prog.p2a = _p2a
Prog.p2b = _p2b
Prog.p3 = _p3
Prog.p4 = _p4
Prog.p5 = _p5
Prog.p6a = _p6a
Prog.p6b = _p6b
Prog.resid_epilogue = _resid_epilogue


def build_prog(S, depth, dbg=(), stop=None):
    return Prog(S, depth, dbg, stop)


def make_in_map(inputs, xs, depth):
    cf, cb = host_consts()
    m = {"x": np.ascontiguousarray(xs), "cf": cf, "cb": cb}
    for k in ("w_in", "conv_w", "dn_norm_w", "attn_sink", "w_up_a", "w_up_b", "w_out", "norm_mix_pre",
              "norm_mix_post", "norm_mlp_pre", "norm_mlp_post", "w_mlp_in", "w_mlp_out"):
        m[k] = np.ascontiguousarray(np.asarray(inputs[k], np.float32)[:depth])
    m["a_log"] = np.ascontiguousarray(np.asarray(inputs["a_log"], np.float32)[:depth].reshape(depth, 16))
    m["dt_bias"] = np.ascontiguousarray(np.asarray(inputs["dt_bias"], np.float32)[:depth].reshape(depth, 16))
    return m


def kernel(**inputs):
    x = np.asarray(inputs["x"], np.float32)
    B, S, _ = x.shape
    depth = inputs["w_in"].shape[0]
    prog = build_prog(S, depth)
    in_maps = [make_in_map(inputs, x[b], depth) for b in range(B)]
    res = run_bass_kernel_spmd(prog.nc, in_maps, core_ids=list(range(B)))
    return np.stack([np.asarray(r["out"], np.float32) for r in res.results], axis=0)
```

```python
import contextlib
import numpy as np
import concourse.bass as bass
import concourse.mybir as mybir

F32 = mybir.dt.float32
BF16 = mybir.dt.bfloat16
ALU = mybir.AluOpType
AF = mybir.ActivationFunctionType
AX = mybir.AxisListType


class Reg:
    __slots__ = ("name", "writers", "readers", "excl")

    def __init__(self, name="", excl=False):
        self.name = name
        self.excl = excl
        self.writers = []
        self.readers = []


class Ctx:
    ENG = ("pe", "act", "dve", "pool", "sp")

    def __init__(self, nc, n_dma_sems=24):
        self.nc = nc
        self.es = contextlib.ExitStack()
        self.eng = {"pe": nc.tensor, "act": nc.scalar, "dve": nc.vector,
                    "pool": nc.gpsimd, "sp": nc.sync}
        self.phase_id = 0
        self.n_dma_sems = n_dma_sems
        self.dma_sems = [self.es.enter_context(nc.semaphore(f"dq{i}")) for i in range(n_dma_sems)]
        self.dma_tot = [0] * n_dma_sems
        self.dma_rr = 0
        self.semsets = [{e: self.es.enter_context(nc.semaphore(f"s{k}_{e}")) for e in self.ENG}
                        for k in range(3)]
        self._new_sems()

    def _new_sems(self):
        self.sem = self.semsets[self.phase_id % 3]
        self.cnt = {e: 0 for e in self.ENG}
        self.seen = {e: {} for e in self.ENG}
        self.pending = {e: [] for e in self.ENG}

    def sb(self, name, shape, dtype):
        self.uid = getattr(self, "uid", 0) + 1
        return self.es.enter_context(self.nc.sbuf_tensor(f"sb{self.uid}_{name}", list(shape), dtype))

    def ps(self, name, shape, dtype=F32):
        self.uid = getattr(self, "uid", 0) + 1
        return self.es.enter_context(self.nc.psum_tensor(f"ps{self.uid}_{name}", list(shape), dtype))

    def _need(self, e, deps):
        need = {}
        for d in deps:
            if d is None:
                continue
            src, val, ph = d
            if isinstance(src, str) and ph != self.phase_id:
                continue
            if isinstance(val, list):
                if val[0] is None:
                    raise RuntimeError(f"dependency on unsignaled op of {src}")
                val = val[0]
            if val > need.get(src, 0):
                need[src] = val
        for src, val in need.items():
            if self.seen[e].get(src, 0) >= val:
                continue
            s = self.sem[src] if isinstance(src, str) else self.dma_sems[src[1]]
            self.eng[e].wait_ge(s, val)
            self.seen[e][src] = val

    def _collect(self, reads, writes, e, pe_accum=False, adds=()):
        deps = []
        for r in reads:
            deps.extend(r.writers)
            if r.excl:
                deps.extend(t for t in r.readers if t[0] != e)
        for w in writes:
            for t in w.writers:
                if not (pe_accum and t[0] == "pe"):
                    deps.append(t)
            deps.extend(w.readers)
        for w in adds:
            deps.extend(w.readers)
        return deps

    def _commit(self, tok, reads, writes, adds):
        for r in reads:
            r.readers.append(tok)
        for w in writes:
            w.writers = [tok]
            w.readers = []
        for w in adds:
            w.writers.append(tok)

    @contextlib.contextmanager
    def scope(self):
        old = self.es
        self.es = contextlib.ExitStack()
        try:
            yield
        finally:
            self.barrier()
            self.es.close()
            self.es = old

    def op(self, e, fn, reads=(), writes=(), sig=True, pe_accum=False, adds=()):
        deps = self._collect(reads, writes, e, pe_accum, adds)
        self._need(e, deps)
        inst = fn()
        if sig:
            self.cnt[e] += 1
            inst.then_inc(self.sem[e], 1)
            tok = (e, self.cnt[e], self.phase_id)
            for p in self.pending[e]:
                p[0] = self.cnt[e]
            self.pending[e] = []
        else:
            cell = [None]
            self.pending[e].append(cell)
            tok = (e, cell, self.phase_id)
        self._commit(tok, reads, writes, adds)
        return inst

    def dma(self, q, out, in_, reads=(), writes=(), adds=(), **kw):
        deps = self._collect(reads, writes, q, False, adds)
        i = self.dma_rr
        self.dma_rr = (self.dma_rr + 1) % self.n_dma_sems
        if self.dma_tot[i] > 0:
            deps.append((("d", i), self.dma_tot[i], 0))
        self._need(q, deps)
        self.dma_tot[i] += 16
        inst = self.eng[q].dma_start(out=out, in_=in_, **kw)
        inst.then_inc(self.dma_sems[i], 16)
        tok = (("d", i), self.dma_tot[i], 0)
        self._commit(tok, reads, writes, adds)
        return inst

    def drain_dmas(self, e="sp"):
        deps = [(("d", i), t, 0) for i, t in enumerate(self.dma_tot) if t > 0]
        self._need(e, deps)

    def barrier(self):
        for e in self.ENG:
            deps = [(("d", i), t, 0) for i, t in enumerate(self.dma_tot) if t > 0]
            for o in self.ENG:
                if o != e and self.cnt[o] > 0:
                    deps.append((o, self.cnt[o], self.phase_id))
            self._need(e, deps)
        old_seen = self.seen
        self.phase_id += 1
        self._new_sems()
        nxt = self.semsets[(self.phase_id + 1) % 3]
        for e in self.ENG:
            self.eng[e].sem_clear(nxt[e])
        for e in self.ENG:
            self.seen[e] = {k: v for k, v in old_seen[e].items() if not isinstance(k, str)}

    def close(self):
        self.es.close()


import ml_dtypes
from concourse.bass_utils import run_bass_kernel_spmd

F32R = mybir.dt.float32r
D = 1024
INC = 4896
DFF = 4096
C_Q, C_K, C_V, C_Z, C_G, C_AQ, C_AK, C_AV, C_GA, C_GB = 0, 512, 1024, 1536, 2048, 2080, 2592, 2720, 2848, 3872
EPS = 1e-6
NEG = -30000.0


class Rot:
    def __init__(self, c, name, n, shape, dtype, psum=False):
        self.items = []
        for i in range(n):
            t = c.ps(f"{name}{i}", shape, dtype) if psum else c.sb(f"{name}{i}", shape, dtype)
            self.items.append((t, Reg(f"{name}{i}", excl=psum)))
        self.i = 0

    def next(self):
        it = self.items[self.i]
        self.i = (self.i + 1) % len(self.items)
        return it


def host_consts():
    p = np.arange(128)[:, None]
    f = np.arange(128)[None, :]
    same = (p // 64) == (f // 64)
    cf = np.zeros((128, 34, 128), np.float32)
    cf[:, 0] = np.where(same & (f >= p), 0.0, NEG)
    cf[:, 1] = np.where(same & (p > f), 0.0, NEG)
    cf[:, 2] = np.where(same & (p >= f), 0.0, NEG)
    cf[:, 3] = np.where(same & (f > p), 0.0, NEG)
    cf[:, 4] = (same & (p <= f)).astype(np.float32)
    cf[:, 5] = (same & (p >= f)).astype(np.float32)
    cf[:, 6] = same.astype(np.float32)
    cf[:, 7] = (p == f).astype(np.float32)
    cf[:, 8, 0] = (np.arange(128) < 64)
    cf[:, 8, 1] = (np.arange(128) >= 64)
    for k in range(6):
        b = 1 << k
        pat = ((p // (2 * b)) == (f // (2 * b))) & ((p // b) != (f // b))
        cf[:, 10 + k] = (pat & (p > f)).astype(np.float32)
        cf[:, 16 + k] = (pat & (p < f)).astype(np.float32)
        cf[:, 22 + k] = -cf[:, 10 + k]
        cf[:, 28 + k] = -cf[:, 16 + k]
    slopes = np.exp2(-8.0 * np.arange(1, 9, dtype=np.float32) / 8)
    cb = np.zeros((128, 3, 8, 128), np.float32)
    for kt, off in enumerate((-128, 0, 128)):
        dist = (f - (p + off))
        valid = np.abs(dist) <= 128
        for h in range(8):
            cb[:, kt, h, :] = np.where(valid, -8.0 * slopes[h] * np.abs(dist), NEG * 8)
    return cf, cb.astype(ml_dtypes.bfloat16)


class Prog:
    def __init__(self, S, depth, dbg=(), stop=None):
        self.S, self.depth, self.dbg, self.stop = S, depth, set(dbg), stop
        nc = self.nc = bass.Bass("TRN2", target_bir_lowering=False)
        self.nph = 0
        dt = lambda n, sh, d=F32: nc.dram_tensor(n, list(sh), d, kind="ExternalInput").ap()
        L = depth
        self.x = dt("x", [S, D])
        self.w_in = dt("w_in", [L, D, INC])
        self.conv_w = dt("conv_w", [L, 5, 1536])
        self.a_log = dt("a_log", [L, 16])
        self.dt_bias = dt("dt_bias", [L, 16])
        self.dn_norm_w = dt("dn_norm_w", [L, 64])
        self.attn_sink = dt("attn_sink", [L, 8])
        self.w_up_a = dt("w_up_a", [L, 512, D])
        self.w_up_b = dt("w_up_b", [L, 512, D])
        self.w_out = dt("w_out", [L, D, D])
        self.n_mix_pre = dt("norm_mix_pre", [L, D])
        self.n_mix_post = dt("norm_mix_post", [L, D])
        self.n_mlp_pre = dt("norm_mlp_pre", [L, D])
        self.n_mlp_post = dt("norm_mlp_post", [L, D])
        self.w1 = dt("w_mlp_in", [L, D, DFF])
        self.w2 = dt("w_mlp_out", [L, DFF, D])
        self.cf_d = dt("cf", [128, 34, 128])
        self.cb_d = dt("cb", [128, 3, 8, 128], BF16)
        self.out = nc.dram_tensor("out", [S, D], F32, kind="ExternalOutput").ap()
        self.c = Ctx(nc)
        self.build()
        self.c.close()

    def dump(self, name, ap, reg, dtype=F32):
        if name not in self.dbg:
            return
        shape = list(ap.shape)
        d = self.nc.dram_tensor("dbg_" + name, shape, dtype, kind="ExternalOutput").ap()
        self.c.dma("sp", d, ap, reads=[reg])

    def scr(self, name, shape, dtype=F32):
        kind = "ExternalOutput" if name in self.dbg else "Internal"
        return self.nc.dram_tensor(name, list(shape), dtype, kind=kind).ap()

    def build(self):
        c, nc, S = self.c, self.nc, self.S
        self.cf = c.sb("cf", [128, 34, 128], F32)
        self.Rcf = Reg("cf")
        c.dma("sp", self.cf[:], self.cf_d[:, :, :], writes=[self.Rcf])
        self.identb = c.sb("identb", [128, 128], BF16)
        self.Ridb = Reg("identb")
        c.op("dve", lambda: nc.vector.tensor_copy(out=self.identb[:], in_=self.cf[:, 7, :]),
             reads=[self.Rcf], writes=[self.Ridb])
        self.qkvT = self.scr("qkvT", [1536, S])
        self.aqT = self.scr("aqT", [512, S], BF16)
        self.akT = self.scr("akT", [128, S], BF16)
        self.sgT = self.scr("sgT", [2048, S])
        self.ztok = self.scr("ztok", [S, 512])
        self.gtok = self.scr("gtok", [S, 32])
        self.avtok = self.scr("avtok", [S, 128], BF16)
        NT = S // 128
        self.ts_d = self.scr("ts_d", [S, 80])
        self.gcT_d = self.scr("gcT_d", [16, S])
        self.egl_d = self.scr("egl_d", [NT * 32])
        self.qT_d = self.scr("qT_d", [512, S], BF16)
        self.kT_d = self.scr("kT_d", [512, S], BF16)
        self.ktok_d = self.scr("ktok_d", [S, 512])
        self.vtok_d = self.scr("vtok_d", [S, 512])
        self.oaT_d = self.scr("oaT_d", [512, S], BF16)
        self.odir_d = [self.scr("ofwd_d", [S, 512]), self.scr("obwd_d", [S, 512])]
        self.obT_d = self.scr("obT_d", [512, S], BF16)
        self.uT_d = self.scr("uT_d", [DFF, S], BF16)
        self.xa = self.scr("xa", [S, D])
        self.xb = self.scr("xb", [S, D])
        for l in range(self.depth):
            xin = self.x if l == 0 else self.xb
            x1 = self.xa
            x2 = self.out if l == self.depth - 1 else self.xb
            for fn in (lambda: self.p1(l, xin), lambda: self.p2a(l), lambda: self.p2b(l), lambda: self.p3(l),
                       lambda: self.p4(l), lambda: self.p5(l, xin, x1), lambda: self.p6a(l, x1),
                       lambda: self.p6b(l, x1, x2)):
                if self.phase(fn):
                    return

    def phase(self, fn):
        with self.c.scope():
            fn()
        self.nph += 1
        return self.stop is not None and self.nph >= self.stop

    def rstd_of(self, ss_ap, out_ap, Rss, n):
        c, nc = self.c, self.nc
        c.op("act", lambda: nc.scalar.activation(out=out_ap, in_=ss_ap, func=AF.Sqrt,
                                                 bias=self.epsb[:, 0:1], scale=1.0 / n),
             reads=[Rss, self.Reps], writes=[Rss])
        c.op("dve", lambda: nc.vector.reciprocal(out=out_ap, in_=out_ap), reads=[Rss], writes=[Rss])

    def load_wcast(self, dst, src, Rlist_append, ncols):
        c = self.c
        for c0 in range(0, ncols, 2048):
            c1 = min(ncols, c0 + 2048)
            r = Reg("w")
            c.dma("pool", dst[:, c0:c1], src[:, c0:c1], writes=[r])
            Rlist_append.append(r)

    def norm_to_hT(self, X, r0, l, gammaT, hT, RhT, tt, xts, ss, Rss, k, pT, RpT):
        c, nc = self.c, self.nc
        xt, Rxt = xts.next()
        c.dma("sp", xt[:], X[r0:r0 + 128, :], writes=[Rxt])
        junk, Rj = self.junk
        c.op("act", lambda: nc.scalar.activation(out=junk[:], in_=xt[:], func=AF.Square,
                                                 accum_out=ss[:, 2 * k:2 * k + 1]),
             reads=[Rxt], writes=[Rj, Rss])
        self.rstd_of(ss[:, 2 * k:2 * k + 1], ss[:, 2 * k + 1:2 * k + 2], Rss, D)
        xn, Rxn = self.xn
        c.op("dve", lambda: nc.vector.tensor_scalar(out=xn[:], in0=xt[:], scalar1=ss[:, 2 * k + 1:2 * k + 2],
                                                    scalar2=None, op0=ALU.mult),
             reads=[Rxt, Rss], writes=[Rxn])
        for kc in range(8):
            c.op("pe", lambda: nc.tensor.transpose(pT[:, kc, :], xn[:, kc * 128:(kc + 1) * 128], self.identb[:]),
                 reads=[Rxn, self.Ridb], writes=[RpT], pe_accum=True, sig=(kc == 7))
        c.op("dve", lambda: nc.vector.tensor_tensor(
            out=hT[:, :, tt * 128:(tt + 1) * 128], in0=pT[:],
            in1=gammaT[:, :].unsqueeze(2).to_broadcast([128, 8, 128]), op=ALU.mult),
            reads=[RpT, self.Rgam], adds=[RhT])

    def p1(self, l, X):
        c, nc, S = self.c, self.nc, self.S
        NB = S // 512
        self.epsb = c.sb("epsb", [128, 1], F32)
        self.Reps = Reg()
        c.op("pool", lambda: nc.gpsimd.memset(self.epsb[:], EPS), writes=[self.Reps])
        wb = c.sb("wb", [128, 8, INC], BF16)
        Rwb = [[] for _ in range(8)]
        for c0 in range(0, INC, 2048):
            c1 = min(INC, c0 + 2048)
            for kc in range(8):
                r = Reg("w")
                c.dma("pool", wb[:, kc, c0:c1], self.w_in[l, kc * 128:(kc + 1) * 128, c0:c1], writes=[r])
                Rwb[kc].append(r)
        gammaT = c.sb("gammaT", [128, 8], F32)
        self.Rgam = Reg()
        with nc.allow_non_contiguous_dma(reason="tiny gamma transpose"):
            c.dma("sp", gammaT[:], self.n_mix_pre.rearrange("l (c p) -> l p c", p=128)[l], writes=[self.Rgam])
        xts = Rot(c, "xt", 2, [128, D], F32)
        self.junk = (c.sb("junk", [128, D], BF16), Reg())
        self.xn = (c.sb("xn", [128, D], BF16), Reg())
        ss = c.sb("ss", [128, 2 * NB * 4], F32)
        Rss = Reg()
        c.op("pool", lambda: nc.gpsimd.memset(ss[:], 0.0), writes=[Rss])
        pT = c.ps("pT", [128, 8, 128], BF16)
        RpT = Reg("pT", excl=True)
        hTs = Rot(c, "hT", 2, [128, 8, 512], BF16)
        pss = Rot(c, "pf", 4, [128, 512], F32, psum=True)
        stg = Rot(c, "stg", 4, [128, 512], F32)
        stgb = Rot(c, "stgb", 3, [128, 512], BF16)
        stgs = Rot(c, "stgs", 2, [128, 32], F32)
        evi = [0]

        def evac_copy(dst, src, rd, wr):
            e = ("dve", "act")[evi[0] % 2]
            evi[0] += 1
            if e == "dve":
                c.op("dve", lambda: nc.vector.tensor_copy(out=dst, in_=src), reads=rd, writes=wr)
            else:
                c.op("act", lambda: nc.scalar.copy(out=dst, in_=src), reads=rd, writes=wr)

        for b in range(NB):
            hT, RhT = hTs.next()
            RhT.writers = []
            for tt in range(4):
                k = b * 4 + tt
                self.norm_to_hT(X, b * 512 + tt * 128, l, gammaT, hT, RhT, tt, xts, ss, Rss, k, pT, RpT)
            t0 = b * 512
            fm = []
            for i in range(12):
                fm.append((C_Q + i * 128, 128, "copy32", self.qkvT[i * 128:(i + 1) * 128, t0:t0 + 512]))
            for i in range(4):
                fm.append((C_AQ + i * 128, 128, "copy16", self.aqT[i * 128:(i + 1) * 128, t0:t0 + 512]))
            fm.append((C_AK, 128, "copy16", self.akT[:, t0:t0 + 512]))
            for i in range(16):
                fm.append((C_GA + i * 128, 128, "sig", self.sgT[i * 128:(i + 1) * 128, t0:t0 + 512]))
            for (c0, m, kind, dst) in fm:
                ps, Rps = pss.next()
                for kc in range(8):
                    c.op("pe", lambda: nc.tensor.matmul(ps[0:m, :], lhsT=wb[:, kc, c0:c0 + m], rhs=hT[:, kc, :],
                                                        start=(kc == 0), stop=(kc == 7)),
                         reads=Rwb[kc] + [RhT], writes=[Rps], pe_accum=True, sig=(kc == 7))
                if kind == "copy16":
                    st, Rst = stgb.next()
                    evac_copy(st[0:m, :], ps[0:m, :], [Rps], [Rst])
                elif kind == "copy32":
                    st, Rst = stg.next()
                    evac_copy(st[0:m, :], ps[0:m, :], [Rps], [Rst])
                else:
                    st, Rst = stg.next()
                    c.op("act", lambda: nc.scalar.activation(out=st[0:m, :], in_=ps[0:m, :], func=AF.Sigmoid),
                         reads=[Rps], writes=[Rst])
                c.dma("sp", dst, st[0:m, :], reads=[Rst])
            for tt in range(4):
                r0 = t0 + tt * 128
                for (c0, n, kind, dst) in ((C_Z, 512, "copy32", self.ztok[r0:r0 + 128, :]),
                                           (C_G, 32, "small", self.gtok[r0:r0 + 128, :]),
                                           (C_AV, 128, "copy16", self.avtok[r0:r0 + 128, :])):
                    ps, Rps = pss.next()
                    for kc in range(8):
                        c.op("pe", lambda: nc.tensor.matmul(ps[:, 0:n], lhsT=hT[:, kc, tt * 128:(tt + 1) * 128],
                                                            rhs=wb[:, kc, c0:c0 + n], start=(kc == 0), stop=(kc == 7)),
                             reads=Rwb[kc] + [RhT], writes=[Rps], pe_accum=True, sig=(kc == 7))
                    if kind == "copy32":
                        st, Rst = stg.next()
                    elif kind == "small":
                        st, Rst = stgs.next()
                    else:
                        st, Rst = stgb.next()
                    evac_copy(st[:, 0:n], ps[:, 0:n], [Rps], [Rst])
                    c.dma("sp", dst, st[:, 0:n], reads=[Rst])


def _p2a(self, l):
    c, nc, S = self.c, self.nc, self.S
    NT = S // 128
    cf = self.cf
    G = c.sb("G", [128, NT, 32], F32); RG = Reg()
    c.dma("sp", G[:], self.gtok.rearrange("(t p) c -> p t c", p=128), writes=[RG])
    DTB = c.sb("DTB", [128, 16], F32); ALG = c.sb("ALG", [128, 16], F32); Rc = Reg()
    c.dma("sp", DTB[:], self.dt_bias[l:l + 1, :].partition_broadcast(128), adds=[Rc])
    c.dma("sp", ALG[:], self.a_log[l:l + 1, :].partition_broadcast(128), adds=[Rc])
    one = c.sb("one", [128, 1], F32); R1 = Reg()
    c.op("pool", lambda: nc.gpsimd.memset(one[:], 1.0), writes=[R1])
    NR = c.sb("NR", [128, 16], F32); RNR = Reg()
    c.op("act", lambda: nc.scalar.activation(out=NR[:], in_=ALG[:], func=AF.Exp), reads=[Rc], writes=[RNR])
    c.op("dve", lambda: nc.vector.tensor_scalar(out=NR[:], in0=NR[:], scalar1=-1.0, scalar2=None, op0=ALU.mult),
         reads=[RNR], writes=[RNR])
    sh = [128, NT, 16]
    bc = lambda t: t[:, :].unsqueeze(1).to_broadcast(sh)
    A = c.sb("A", sh, F32); RA = Reg()
    AB = c.sb("AB", sh, F32); RAB = Reg()
    GG = c.sb("GG", sh, F32); RGG = Reg()
    BE = c.sb("BE", sh, F32); RBE = Reg()
    LNB = c.sb("LNB", sh, F32); RLNB = Reg()
    c.op("dve", lambda: nc.vector.tensor_tensor(out=A[:], in0=G[:, :, 0:16], in1=bc(DTB), op=ALU.add),
         reads=[RG, Rc], writes=[RA])
    c.op("act", lambda: nc.scalar.activation(out=AB[:], in_=A[:], func=AF.Abs), reads=[RA], writes=[RAB])
    c.op("act", lambda: nc.scalar.activation(out=AB[:], in_=AB[:], func=AF.Exp, scale=-1.0), reads=[RAB], writes=[RAB])
    c.op("act", lambda: nc.scalar.activation(out=AB[:], in_=AB[:], func=AF.Ln, bias=one[:, 0:1]),
         reads=[RAB, R1], writes=[RAB])
    c.op("dve", lambda: nc.vector.tensor_scalar_max(out=A[:], in0=A[:], scalar1=0.0), reads=[RA], writes=[RA])
    c.op("dve", lambda: nc.vector.tensor_tensor(out=A[:], in0=A[:], in1=AB[:], op=ALU.add), reads=[RA, RAB], writes=[RA])
    c.op("dve", lambda: nc.vector.tensor_tensor(out=GG[:], in0=A[:], in1=bc(NR), op=ALU.mult),
         reads=[RA, RNR], writes=[RGG])
    c.op("act", lambda: nc.scalar.activation(out=BE[:], in_=G[:, :, 16:32], func=AF.Sigmoid), reads=[RG], writes=[RBE])
    c.op("act", lambda: nc.scalar.activation(out=LNB[:], in_=BE[:], func=AF.Ln), reads=[RBE], writes=[RLNB])
    bank = lambda n: c.ps(n, [128, 512], F32)
    GCp = bank("GCp")[:, 0:NT * 16].rearrange("p (t c) -> p t c", c=16); RGCp = Reg("GCp", excl=True)
    GLp = bank("GLp")[:, 0:NT * 16].rearrange("p (t c) -> p t c", c=16); RGLp = Reg("GLp", excl=True)
    EGp = bank("EGp")[0:16, 0:NT * 2].rearrange("p (t c) -> p t c", c=2); REGp = Reg("EGp", excl=True)
    for t in range(NT):
        last = (t == NT - 1)
        c.op("pe", lambda: nc.tensor.matmul(GCp[:, t, 0:8], lhsT=cf[:, 4, :], rhs=GG[:, t, 0:8], start=True, stop=True),
             reads=[RGG, self.Rcf], writes=[RGCp], pe_accum=True, sig=False)
        c.op("pe", lambda: nc.tensor.matmul(GCp[:, t, 8:16], lhsT=cf[:, 5, :], rhs=GG[:, t, 8:16], start=True, stop=True),
             reads=[RGG, self.Rcf], writes=[RGCp], pe_accum=True, sig=last)
        c.op("pe", lambda: nc.tensor.matmul(GLp[:, t, :], lhsT=cf[:, 6, :], rhs=GG[:, t, :], start=True, stop=True),
             reads=[RGG, self.Rcf], writes=[RGLp], pe_accum=True, sig=last)
        c.op("pe", lambda: nc.tensor.matmul(EGp[:, t, :], lhsT=GG[:, t, :], rhs=cf[:, 8, 0:2], start=True, stop=True),
             reads=[RGG, self.Rcf], writes=[REGp], pe_accum=True, sig=last)
    TS = c.sb("TS", [128, NT, 80], F32); RTS = Reg()
    TMP = c.sb("TMP", sh, F32); RTMP = Reg()
    c.op("act", lambda: nc.scalar.copy(out=TS[:, :, 0:16], in_=GCp), reads=[RGCp], adds=[RTS])
    c.op("dve", lambda: nc.vector.tensor_tensor(out=TS[:, :, 16:32], in0=GCp, in1=LNB[:], op=ALU.add),
         reads=[RGCp, RLNB], adds=[RTS])
    c.op("pool", lambda: nc.gpsimd.tensor_copy(out=TS[:, :, 32:48], in_=BE[:]), reads=[RBE], adds=[RTS])
    c.op("act", lambda: nc.scalar.activation(out=TMP[:], in_=GCp, func=AF.Exp), reads=[RGCp], writes=[RTMP])
    c.op("dve", lambda: nc.vector.tensor_tensor(out=TS[:, :, 48:64], in0=TMP[:], in1=BE[:], op=ALU.mult),
         reads=[RTMP, RBE], adds=[RTS])
    TM2 = c.sb("TM2", sh, F32); RTM2 = Reg()
    c.op("act", lambda: nc.scalar.copy(out=TM2[:], in_=GCp), reads=[RGCp], writes=[RTM2])
    c.op("dve", lambda: nc.vector.tensor_tensor(out=TM2[:], in0=GLp, in1=TM2[:], op=ALU.subtract),
         reads=[RGLp, RTM2], writes=[RTM2])
    c.op("act", lambda: nc.scalar.activation(out=TS[:, :, 64:80], in_=TM2[:], func=AF.Exp), reads=[RTM2], adds=[RTS])
    c.dma("sp", self.ts_d.rearrange("(t p) c -> p t c", p=128), TS[:], reads=[RTS])
    EGs = c.sb("EGs", [16, NT, 2], F32); REGs = Reg()
    c.op("act", lambda: nc.scalar.activation(out=EGs[:], in_=EGp, func=AF.Exp), reads=[REGp], writes=[REGs])
    with nc.allow_non_contiguous_dma(reason="tiny egl transpose"):
        for d in range(2):
            for ch in range(2):
                c.dma("sp", self.egl_d.rearrange("(t d c h) -> d c h t", d=2, c=2, h=8)[d, ch],
                      EGs[d * 8:(d + 1) * 8, :, ch], reads=[REGs])
    gcs = c.sb("gcs", [8, 2, S], F32); Rgcs = Reg()
    pgs = Rot(c, "pg", 2, [8, 512], F32, psum=True)
    for d in range(2):
        for g4 in range(0, NT, 4):
            ps, Rps = pgs.next()
            n = min(4, NT - g4)
            for i in range(n):
                t = g4 + i
                c.op("pe", lambda: nc.tensor.matmul(ps[:, i * 128:(i + 1) * 128], lhsT=GG[:, t, d * 8:(d + 1) * 8],
                                                    rhs=cf[:, 4 + d, :], start=True, stop=True),
                     reads=[RGG, self.Rcf], writes=[Rps], pe_accum=True, sig=(i == n - 1))
            c.op("dve", lambda: nc.vector.tensor_copy(out=gcs[:, d, g4 * 128:(g4 + n) * 128], in_=ps[:, 0:n * 128]),
                 reads=[Rps], adds=[Rgcs])
    for d in range(2):
        c.dma("sp", self.gcT_d[d * 8:(d + 1) * 8, :], gcs[:, d, :], reads=[Rgcs])


def _p2b(self, l):
    c, nc, S = self.c, self.nc, self.S
    NT = S // 128
    cf = self.cf
    epsb = c.sb("epsb2", [128, 1], F32); Reps = Reg()
    c.op("pool", lambda: nc.gpsimd.memset(epsb[:], EPS), writes=[Reps])
    CW = c.sb("CW", [128, 12, 5], F32); RCW = Reg()
    with nc.allow_non_contiguous_dma(reason="tiny conv weight transpose"):
        for ci in range(12):
            c.dma("sp", CW[:, ci, :], self.conv_w[l, :, ci * 128:(ci + 1) * 128].rearrange("j p -> p j"), adds=[RCW])
    Us = Rot(c, "U", 2, [128, S + 4], F32)
    for (U, RU) in Us.items:
        c.op("pool", lambda: nc.gpsimd.memset(U[:, 0:2], 0.0), adds=[RU])
        c.op("pool", lambda: nc.gpsimd.memset(U[:, S + 2:S + 4], 0.0), adds=[RU])
    ACCs = Rot(c, "ACC", 2, [128, S], F32)
    SQ = c.sb("SQ", [128, S], F32); RSQ = Reg()
    NRMs = Rot(c, "NRM", 2, [128, S], F32)
    NB16s = Rot(c, "NB16", 2, [128, S], BF16)
    TKs = Rot(c, "TK", 2, [128, NT, 128], F32)
    pls = Rot(c, "pl", 2, [128, 512], F32, psum=True)
    pts = Rot(c, "pt", 2, [128, 4, 128], F32, psum=True)
    for ci in range(12):
        U, RU = Us.next()
        ACC, RACC = ACCs.next()
        NRM, RNRM = NRMs.next()
        NB16, RNB = NB16s.next()
        TK, RTK = TKs.next()
        RNRM.writers = []
        RTK.writers = []
        e = "dve"
        eng = nc.vector
        c.dma("sp", U[:, 2:S + 2], self.qkvT[ci * 128:(ci + 1) * 128, :], adds=[RU])
        c.op(e, lambda: eng.tensor_scalar(out=ACC[:], in0=U[:, 0:S], scalar1=CW[:, ci, 0:1], scalar2=None, op0=ALU.mult),
             reads=[RU, RCW], writes=[RACC])
        for j in range(1, 5):
            c.op(e, lambda: eng.scalar_tensor_tensor(out=ACC[:], in0=U[:, j:j + S], scalar=CW[:, ci, j:j + 1], in1=ACC[:],
                                                     op0=ALU.mult, op1=ALU.add),
                 reads=[RU, RCW, RACC], writes=[RACC])
        RU.writers = []
        c.op("act", lambda: nc.scalar.activation(out=ACC[:], in_=ACC[:], func=AF.Silu), reads=[RACC], writes=[RACC])
        if ci < 8:
            c.op("act", lambda: nc.scalar.activation(out=SQ[:], in_=ACC[:], func=AF.Square), reads=[RACC], writes=[RSQ])
            for b in range(0, S, 512):
                ps, Rps = pls.next()
                c.op("pe", lambda: nc.tensor.matmul(ps[:], lhsT=cf[:, 6, :], rhs=SQ[:, b:b + 512], start=True, stop=True),
                     reads=[RSQ, self.Rcf], writes=[Rps])
                c.op("act", lambda: nc.scalar.activation(out=NRM[:, b:b + 512], in_=ps[:], func=AF.Ln, bias=epsb[:, 0:1]),
                     reads=[Rps, Reps], adds=[RNRM])
            c.op("act", lambda: nc.scalar.activation(out=NRM[:], in_=NRM[:], func=AF.Exp, scale=-0.5), reads=[RNRM], writes=[RNRM])
            c.op("dve", lambda: nc.vector.tensor_tensor(out=NRM[:], in0=NRM[:], in1=ACC[:], op=ALU.mult),
                 reads=[RNRM, RACC], writes=[RNRM])
            src, Rsrc = NRM, RNRM
            c.op("act", lambda: nc.scalar.copy(out=NB16[:], in_=NRM[:]), reads=[RNRM], writes=[RNB])
            dstT = self.qT_d if ci < 4 else self.kT_d
            c.dma("sp", dstT[(ci % 4) * 128:(ci % 4 + 1) * 128, :], NB16[:], reads=[RNB])
        else:
            src, Rsrc = ACC, RACC
        if ci >= 4:
            for g4 in range(0, NT, 4):
                ps, Rps = pts.next()
                n = min(4, NT - g4)
                for i in range(n):
                    t = g4 + i
                    c.op("pe", lambda: nc.tensor.transpose(ps[:, i, :], src[:, t * 128:(t + 1) * 128], cf[:, 7, :]),
                         reads=[Rsrc, self.Rcf], writes=[Rps], pe_accum=True, sig=(i == n - 1))
                ev = "dve" if (g4 // 4) % 2 == 0 else "act"
                if ev == "dve":
                    c.op("dve", lambda: nc.vector.tensor_copy(out=TK[:, g4:g4 + n, :], in_=ps[:, 0:n, :]), reads=[Rps], adds=[RTK])
                else:
                    c.op("act", lambda: nc.scalar.copy(out=TK[:, g4:g4 + n, :], in_=ps[:, 0:n, :]), reads=[Rps], adds=[RTK])
            dst = self.ktok_d if ci < 8 else self.vtok_d
            c.dma("sp", dst.rearrange("(t p) c -> p t c", p=128)[:, :, (ci % 4) * 128:(ci % 4 + 1) * 128], TK[:], reads=[RTK])


def _p3(self, l):
    c, nc, S = self.c, self.nc, self.S
    NT = S // 128
    cf = self.cf
    H8 = [128, 8, 128]
    TS = c.sb("TS3", [128, NT, 80], F32); RTS = Reg()
    c.dma("sp", TS[:], self.ts_d.rearrange("(t p) c -> p t c", p=128), writes=[RTS])
    EGLB = c.sb("EGLB", [64, NT * 32], F32); REGL = Reg()
    c.dma("sp", EGLB[:], self.egl_d.partition_broadcast(64), writes=[REGL])
    epsb = c.sb("epsb3", [128, 1], F32); Reps = Reg()
    c.op("pool", lambda: nc.gpsimd.memset(epsb[:], EPS), writes=[Reps])
    Sst = [c.sb(f"Sst{d}", [64, 8, 64], F32) for d in range(2)]
    RS = [Reg(f"S{d}") for d in range(2)]
    for d in range(2):
        c.op("pool", lambda: nc.gpsimd.memset(Sst[d][:], 0.0), writes=[RS[d]])

    PB = Rot(c, "PB", 8, [128, 512], F32, psum=True)
    cfb = c.sb("cfb", [128, 13, 128], BF16); Rcfb = Reg()
    c.op("dve", lambda: nc.vector.tensor_copy(out=cfb[:, 0:12, :], in_=cf[:, 22:34, :]), reads=[self.Rcf], adds=[Rcfb])
    c.op("dve", lambda: nc.vector.tensor_copy(out=cfb[:, 12, :], in_=cf[:, 7, :]), reads=[self.Rcf], adds=[Rcfb])
    with c.scope():
        class W:
            pass
        ws = [[None, None], [None, None]]
        for d in range(2):
          for sl_ in range(2):
            w = W()
            tag = f"{d}{sl_}"
            mk = lambda n, sh, dt: (c.sb(f"{n}{tag}", sh, dt), Reg(f"{n}{tag}"))
            mk2 = lambda n, sh, dt: (c.sb(f"{n}{tag}", sh, dt), [Reg(f"{n}{tag}a"), Reg(f"{n}{tag}b")])
            w.kT, w.RkT = mk("kT", [64, 8, 128], BF16)
            w.qT, w.RqT = mk("qT", [64, 8, 128], BF16)
            w.ktok, w.Rktok = mk("ktok", [128, 8, 64], F32)
            w.vtok, w.Rvtok = mk("vtok", [128, 8, 64], F32)
            w.OS, w.ROS = w.ktok, w.Rktok
            w.OP, w.ROP = w.vtok, w.Rvtok
            w.GR, w.RGR = mk("GR", H8, F32)
            w.TA, w.RTA = w.GR, w.RGR
            w.NN = w.GR[0:64].rearrange("p h i -> p (h i)").rearrange("p (g e) -> p g e", e=64)
            w.RNN = w.RGR
            w.TL, w.RTL = mk("TL", H8, F32)
            w.MT = w.TL[0:64].rearrange("p h i -> p (h i)").rearrange("p (g e) -> p g e", e=64)
            w.RMT = w.RTL
            w.QT, w.RQT = mk("QT", [64, 8, 128], F32)
            w.AT, _ = mk("AT", H8, BF16)
            w.RAT = [Reg(f"AT{tag}a"), Reg(f"AT{tag}b")]
            w.L = [mk2("L0_", H8, BF16)]
            w.Y6b = w.L[0]
            w.U = [mk2("U0_", H8, BF16)]
            w.Y = [mk2("Y0_", H8, BF16)]
            w.Dm = mk2("Dm", H8, BF16)
            w.DTm = mk2("DTm", H8, BF16)
            w.Xm = mk2("Xm", H8, BF16)
            w.NU = [mk2(f"NU{i}_", H8, BF16) for i in range(2)]
            w.KH, w.RKH = mk("KH", [128, 8, 64], BF16)
            ws[d][sl_] = w

        def bc8(ap2, n):
            return ap2.unsqueeze(2).to_broadcast([ap2.shape[0], 8, n])

        def mask(mi):
            return cf[:, mi, :].unsqueeze(1).to_broadcast(H8)

        def ps_view(db, dtype=None):
            return db[:, :].rearrange("p (h i) -> p h i", i=128)


        def v4(bank, np_=128):
            return bank[0:np_, :].rearrange("p (h i) -> p h i", i=128)

        def v8(bank, np_=128):
            return bank[0:np_, :].rearrange("p (h e) -> p h e", e=64)

        def steps(d, t, slot):
            w = ws[d][slot]
            sl = slice(t * 128, (t + 1) * 128)
            c.dma("sp", w.kT[:], self.kT_d.rearrange("(h e) s -> e h s", e=64)[:, :, sl], writes=[w.RkT])
            c.dma("sp", w.qT[:], self.qT_d.rearrange("(h e) s -> e h s", e=64)[:, :, sl], writes=[w.RqT])
            c.dma("sp", w.ktok[:].rearrange("p h e -> p (h e)"), self.ktok_d[sl, :], writes=[w.Rktok])
            c.dma("sp", w.vtok[:].rearrange("p h e -> p (h e)"), self.vtok_d[sl, :], writes=[w.Rvtok])
            c.dma("sp", w.GR[:], self.gcT_d[d * 8:(d + 1) * 8, sl].partition_broadcast(128), writes=[w.RGR])
            for _ in range(3):
                yield
            gcb = TS[:, t, 0 + d * 8:8 + d * 8]
            gcbb = TS[:, t, 16 + d * 8:24 + d * 8]
            beta = TS[:, t, 32 + d * 8:40 + d * 8]
            beg = TS[:, t, 48 + d * 8:56 + d * 8]
            dk = TS[:, t, 64 + d * 8:72 + d * 8]
            c.op("act", lambda: nc.scalar.activation(out=w.TL[0:64], in_=w.GR[0:64], func=AF.Exp), reads=[w.RGR], writes=[w.RTL])
            c.op("dve", lambda: nc.vector.scalar_tensor_tensor(out=w.QT[:], in0=w.qT[:], scalar=0.125, in1=w.TL[0:64],
                                                               op0=ALU.mult, op1=ALU.mult),
                 reads=[w.RqT, w.RTL], writes=[w.RQT])
            c.op("pool", lambda: nc.gpsimd.tensor_tensor(out=w.TL[:], in0=bc8(gcbb, 128), in1=w.GR[:], op=ALU.subtract),
                 reads=[w.RGR, RTS], writes=[w.RTL])
            c.op("dve", lambda: nc.vector.tensor_tensor(out=w.TA[:], in0=w.GR[:], in1=bc8(gcb, 128), op=ALU.subtract),
                 reads=[w.RGR, RTS], writes=[w.RTA])
            c.op("dve", lambda: nc.vector.tensor_tensor(out=w.TL[:], in0=w.TL[:], in1=mask(1 if d == 0 else 3), op=ALU.add),
                 reads=[w.RTL, self.Rcf], writes=[w.RTL])
            c.op("dve", lambda: nc.vector.tensor_tensor(out=w.TA[:], in0=w.TA[:], in1=mask(0 if d == 0 else 2), op=ALU.add),
                 reads=[w.RTA, self.Rcf], writes=[w.RTA])
            c.op("act", lambda: nc.scalar.activation(out=w.TL[:], in_=w.TL[:], func=AF.Exp), reads=[w.RTL], writes=[w.RTL])
            c.op("act", lambda: nc.scalar.activation(out=w.TA[:], in_=w.TA[:], func=AF.Exp), reads=[w.RTA], writes=[w.RTA])
            Y0, RY0 = w.Y[0]
            for hf in range(2):
                hs = slice(hf * 4, hf * 4 + 4)
                c.op("pool", lambda: nc.gpsimd.tensor_tensor(out=Y0[:, hs, 0:64], in0=w.vtok[:, hs, :], in1=bc8(beta, 64)[:, hs, :], op=ALU.mult),
                     reads=[w.Rvtok, RTS], adds=[RY0[hf]])
                c.op("pool", lambda: nc.gpsimd.tensor_tensor(out=Y0[:, hs, 64:128], in0=w.ktok[:, hs, :], in1=bc8(beg, 64)[:, hs, :], op=ALU.mult),
                     reads=[w.Rktok, RTS], adds=[RY0[hf]])
            c.op("pool", lambda: nc.gpsimd.tensor_tensor(out=w.KH[:], in0=w.ktok[:], in1=bc8(dk, 64), op=ALU.mult),
                 reads=[w.Rktok, RTS], writes=[w.RKH])
            for _ in range(5):
                yield
            L0, RL0 = w.L[0]
            for hf in range(2):
                hs = slice(hf * 4, hf * 4 + 4)
                bk, Rbk = PB.next()
                for h in range(hf * 4, hf * 4 + 4):
                    c.op("pe", lambda: nc.tensor.matmul(v4(bk)[:, h % 4, :], lhsT=w.kT[:, h, :], rhs=w.kT[:, h, :], start=True, stop=True),
                         reads=[w.RkT], writes=[Rbk], pe_accum=True, sig=(h % 4 == 3))
                c.op("dve", lambda: nc.vector.tensor_tensor(out=L0[:, hs, :], in0=v4(bk), in1=w.TL[:, hs, :], op=ALU.mult),
                     reads=[Rbk, w.RTL], writes=[RL0[hf]])
                bq, Rbq = PB.next()
                for h in range(hf * 4, hf * 4 + 4):
                    c.op("pe", lambda: nc.tensor.matmul(v4(bq)[:, h % 4, :], lhsT=w.kT[:, h, :], rhs=w.qT[:, h, :], start=True, stop=True),
                         reads=[w.RkT, w.RqT], writes=[Rbq], pe_accum=True, sig=(h % 4 == 3))
                c.op("dve", lambda: nc.vector.scalar_tensor_tensor(out=w.AT[:, hs, :], in0=v4(bq), scalar=0.125, in1=w.TA[:, hs, :],
                                                                   op0=ALU.mult, op1=ALU.mult),
                     reads=[Rbq, w.RTA], writes=[w.RAT[hf]])
                yield
            if d == 0 and t == 0:
                self.dump("AT", w.AT[:], w.RAT[1], BF16); self.dump("L0", L0[:], RL0[1], BF16)
            U0, RU0 = w.U[0]
            for hf in range(2):
                hs = slice(hf * 4, hf * 4 + 4)
                bk, Rbk = PB.next()
                puv = bk[:, :].bitcast(BF16)[:, 0:512].rearrange("p (h i) -> p h i", i=128)
                for h in range(hf * 4, hf * 4 + 4):
                    c.op("pe", lambda: nc.tensor.transpose(puv[:, h % 4, :], L0[:, h, :], self.identb[:]),
                         reads=[RL0[hf], self.Ridb], writes=[Rbk], pe_accum=True, sig=(h % 4 == 3))
                c.op("act", lambda: nc.scalar.copy(out=U0[:, hs, :], in_=puv), reads=[Rbk], writes=[RU0[hf]])
            yield
            Dm, RDm = w.Dm
            DTm, RDTm = w.DTm
            Xm, RXm = w.Xm
            Yc, RYc = w.Y[0]
            nL = (lambda k: 22 + k) if d == 0 else (lambda k: 28 + k)
            nU = (lambda k: 28 + k) if d == 0 else (lambda k: 22 + k)
            m4 = lambda mi: cf[:, mi, :].unsqueeze(1).to_broadcast([128, 4, 128])
            m4b = lambda mi: cfb[:, mi - 22, :].unsqueeze(1).to_broadcast([128, 4, 128])
            i4b = cfb[:, 12, :].unsqueeze(1).to_broadcast([128, 4, 128])
            for hf in range(2):
                hs = slice(hf * 4, hf * 4 + 4)
                c.op("dve", lambda: nc.vector.tensor_tensor(out=Dm[:, hs, :], in0=L0[:, hs, :], in1=m4b(nL(0)), op=ALU.mult),
                     reads=[RL0[hf], Rcfb], writes=[RDm[hf]])
                c.op("dve", lambda: nc.vector.tensor_tensor(out=Dm[:, hs, :], in0=Dm[:, hs, :], in1=i4b, op=ALU.add),
                     reads=[RDm[hf], Rcfb], writes=[RDm[hf]])
                c.op("dve", lambda: nc.vector.tensor_tensor(out=DTm[:, hs, :], in0=U0[:, hs, :], in1=m4b(nU(0)), op=ALU.mult),
                     reads=[RU0[hf], Rcfb], writes=[RDTm[hf]])
                c.op("dve", lambda: nc.vector.tensor_tensor(out=DTm[:, hs, :], in0=DTm[:, hs, :], in1=i4b, op=ALU.add),
                     reads=[RDTm[hf], Rcfb], writes=[RDTm[hf]])
            yield
            for k in range(1, 6):
                NU, RNU = w.NU[k % 2]
                for hf in range(2):
                    hs = slice(hf * 4, hf * 4 + 4)
                    hh = range(hf * 4, hf * 4 + 4)
                    c.op("dve", lambda: nc.vector.tensor_tensor(out=NU[:, hs, :], in0=U0[:, hs, :], in1=m4b(nU(k)), op=ALU.mult),
                         reads=[RU0[hf], Rcfb], writes=[RNU[hf]])
                    b1, Rb1 = PB.next()
                    for h in hh:
                        c.op("pe", lambda: nc.tensor.matmul(v4(b1)[:, h % 4, :], lhsT=NU[:, h, :], rhs=Dm[:, h, :], start=True, stop=True),
                             reads=[RNU[hf], RDm[hf]], writes=[Rb1], pe_accum=True, sig=(h % 4 == 3))
                    c.op("dve", lambda: nc.vector.tensor_tensor(out=Xm[:, hs, :], in0=v4(b1), in1=m4(7), op=ALU.add),
                         reads=[Rb1, self.Rcf], writes=[RXm[hf]])
                    b3, Rb3 = PB.next()
                    for h in hh:
                        c.op("pe", lambda: nc.tensor.matmul(v4(b3)[:, h % 4, :], lhsT=DTm[:, h, :], rhs=Xm[:, h, :], start=True, stop=True),
                             reads=[RDTm[hf], RXm[hf]], writes=[Rb3], pe_accum=True, sig=(h % 4 == 3))
                    b4, Rb4 = PB.next()
                    for h in hh:
                        c.op("pe", lambda: nc.tensor.matmul(v4(b4)[:, h % 4, :], lhsT=Xm[:, h, :], rhs=DTm[:, h, :], start=True, stop=True),
                             reads=[RDTm[hf], RXm[hf]], writes=[Rb4], pe_accum=True, sig=(h % 4 == 3))
                    c.op("act", lambda: nc.scalar.copy(out=Dm[:, hs, :], in_=v4(b3)), reads=[Rb3], writes=[RDm[hf]])
                    c.op("act", lambda: nc.scalar.copy(out=DTm[:, hs, :], in_=v4(b4)), reads=[Rb4], writes=[RDTm[hf]])
                    yield
            Y6, RY6 = w.Y6b
            for hf in range(2):
                hs = slice(hf * 4, hf * 4 + 4)
                bk, Rbk = PB.next()
                for h in range(hf * 4, hf * 4 + 4):
                    c.op("pe", lambda: nc.tensor.matmul(v4(bk)[:, h % 4, :], lhsT=DTm[:, h, :], rhs=Yc[:, h, :], start=True, stop=True),
                         reads=[RDTm[hf], RYc[hf]], writes=[Rbk], pe_accum=True, sig=(h % 4 == 3))
                c.op("act", lambda: nc.scalar.copy(out=Y6[:, hs, :], in_=v4(bk)), reads=[Rbk], writes=[RY6[hf]])
            yield
            if d == 0 and t == 0:
                self.dump("Y6", Y6[:], RY6[1], BF16)
            bk, Rbk = PB.next()
            for h in range(8):
                c.op("pe", lambda: nc.tensor.matmul(v8(bk)[:, h, :], lhsT=w.AT[:, h, :], rhs=Y6[:, h, 0:64], start=True, stop=True),
                     reads=[w.RAT[h // 4], RY6[h // 4]], writes=[Rbk], pe_accum=True, sig=(h == 7))
            c.op("act", lambda: nc.scalar.copy(out=w.OP[:], in_=v8(bk)), reads=[Rbk], writes=[w.ROP])
            yield
            for hf in range(2):
                hs = slice(hf * 4, hf * 4 + 4)
                bk, Rbk = PB.next()
                for h in range(hf * 4, hf * 4 + 4):
                    c.op("pe", lambda: nc.tensor.matmul(v4(bk, 64)[:, h % 4, :], lhsT=Y6[:, h, 64:128], rhs=w.AT[:, h, :], start=True, stop=True),
                         reads=[w.RAT[hf], RY6[hf]], writes=[Rbk], pe_accum=True, sig=(h % 4 == 3))
                c.op("dve", lambda: nc.vector.tensor_tensor(out=w.QT[:, hs, :], in0=w.QT[:, hs, :], in1=v4(bk, 64), op=ALU.subtract),
                     reads=[w.RQT, Rbk], writes=[w.RQT])
            yield
            eg = EGLB[:, (t * 2 + d) * 16:(t * 2 + d + 1) * 16]
            c.op("pool", lambda: nc.gpsimd.tensor_tensor(out=w.MT[:], in0=cf[0:64, 7, 0:64].unsqueeze(1).to_broadcast([64, 16, 64]),
                                                         in1=eg.unsqueeze(2).to_broadcast([64, 16, 64]), op=ALU.mult),
                 reads=[self.Rcf, REGL], writes=[w.RMT])
            for cc in range(2):
                rows = slice(cc * 64, (cc + 1) * 64)
                gs = slice(cc * 8, cc * 8 + 8)
                bk, Rbk = PB.next()
                for h in range(8):
                    c.op("pe", lambda: nc.tensor.matmul(v8(bk, 64)[:, h, :], lhsT=Y6[rows, h, 64:128], rhs=w.KH[rows, h, :],
                                                        start=True, stop=True),
                         reads=RY6 + [w.RKH], writes=[Rbk], pe_accum=True, sig=(h == 7))
                c.op("dve", lambda: nc.vector.tensor_tensor(out=w.MT[:, gs, :], in0=w.MT[:, gs, :], in1=v8(bk, 64), op=ALU.subtract),
                     reads=[w.RMT, Rbk], writes=[w.RMT])
                bk, Rbk = PB.next()
                for h in range(8):
                    c.op("pe", lambda: nc.tensor.matmul(v8(bk, 64)[:, h, :], lhsT=w.KH[rows, h, :], rhs=Y6[rows, h, 0:64],
                                                        start=True, stop=True),
                         reads=RY6 + [w.RKH], writes=[Rbk], pe_accum=True, sig=(h == 7))
                c.op("act", lambda: nc.scalar.copy(out=w.NN[:, gs, :], in_=v8(bk, 64)), reads=[Rbk], writes=[w.RNN])
                yield
            if d == 0 and t == 0:
                self.dump("OP", w.OP[:], w.ROP); self.dump("QT", w.QT[:], w.RQT)
                self.dump("MT", w.MT[:], w.RMT); self.dump("NN", w.NN[:], w.RNN)
            for cc in ((0, 1) if d == 0 else (1, 0)):
                rows = slice(cc * 64, (cc + 1) * 64)
                bs, Rbs = PB.next()
                for h in range(8):
                    c.op("pe", lambda: nc.tensor.matmul(v8(bs, 64)[:, h, :], lhsT=w.MT[:, cc * 8 + h, :], rhs=Sst[d][:, h, :], start=True, stop=True),
                         reads=[w.RMT, RS[d]], writes=[Rbs], pe_accum=True, sig=(h == 7))
                bo, Rbo = PB.next()
                for h in range(8):
                    c.op("pe", lambda: nc.tensor.matmul(v8(bo)[rows, h, :], lhsT=w.QT[:, h, rows], rhs=Sst[d][:, h, :], start=True, stop=True),
                         reads=[w.RQT, RS[d]], writes=[Rbo], pe_accum=True, sig=(h == 7))
                c.op("dve", lambda: nc.vector.tensor_tensor(out=Sst[d][:], in0=v8(bs, 64), in1=w.NN[:, cc * 8:(cc + 1) * 8, :], op=ALU.add),
                     reads=[Rbs, w.RNN], writes=[RS[d]])
                c.op("dve", lambda: nc.vector.tensor_tensor(out=w.OS[rows], in0=v8(bo)[rows], in1=w.OP[rows], op=ALU.add),
                     reads=[Rbo, w.ROP], writes=[w.ROS])
                yield
            c.dma("sp", self.odir_d[d][sl, :], w.OS[:].rearrange("p h e -> p (h e)"), reads=[w.ROS])

        HALF = 15; DEPH = 0
        seqs = [list(range(NT)), list(range(NT - 1, -1, -1))]
        nxt = [0, 0]
        active = [[], []]
        tick = 0
        while any(nxt[d] < NT or active[d] for d in range(2)):
            tick += 1
            for d in range(2):
                if d == 1 and tick <= DEPH:
                    continue
                if nxt[d] < NT and len(active[d]) < 2 and (not active[d] or active[d][-1][1] >= HALF):
                    active[d].append([steps(d, seqs[d][nxt[d]], nxt[d] % 2), 0])
                    nxt[d] += 1
            for pos in range(2):
                for d in range(2):
                    if pos < len(active[d]):
                        ent = active[d][pos]
                        try:
                            next(ent[0])
                            ent[1] += 1
                        except StopIteration:
                            ent[1] = -1
            for d in range(2):
                active[d] = [e for e in active[d] if e[1] >= 0]
    c.drain_dmas("sp")
    NW = c.sb("NW", [128, 64], F32); RNW = Reg()
    c.dma("sp", NW[:], self.dn_norm_w[l:l + 1, :].partition_broadcast(128), writes=[RNW])
    zts = Rot(c, "zt", 2, [128, 8, 64], F32)
    ofs = Rot(c, "of", 2, [128, 8, 64], F32)
    obs = Rot(c, "ob", 2, [128, 8, 64], F32)
    sqs = Rot(c, "sq", 2, [128, 8, 64], F32)
    sst = Rot(c, "sst", 2, [128, 16], F32)
    oas = Rot(c, "oa", 2, [128, 512], BF16)
    oTs = Rot(c, "oT", 2, [128, 4, 128], BF16)
    for t in range(NT):
        zt, Rzt = zts.next()
        sq, Rsq = sqs.next()
        st, Rst = sst.next()
        oa, Roa = oas.next()
        oT, RoT = oTs.next()
        of_, Rof = ofs.next()
        ob_, Rob = obs.next()
        c.dma("sp", of_[:].rearrange("p h e -> p (h e)"), self.odir_d[0][t * 128:(t + 1) * 128, :], writes=[Rof])
        c.dma("sp", ob_[:].rearrange("p h e -> p (h e)"), self.odir_d[1][t * 128:(t + 1) * 128, :], writes=[Rob])
        c.op("pool", lambda: nc.gpsimd.tensor_tensor(out=of_[:], in0=of_[:], in1=ob_[:], op=ALU.add), reads=[Rof, Rob], writes=[Rof])
        ot = of_[:]
        c.dma("sp", zt[:].rearrange("p h e -> p (h e)"), self.ztok[t * 128:(t + 1) * 128, :], writes=[Rzt])
        c.op("act", lambda: nc.scalar.activation(out=sq[:], in_=ot, func=AF.Square), reads=[Rof], writes=[Rsq])
        c.op("dve", lambda: nc.vector.tensor_reduce(out=st[:, 0:8], in_=sq[:], axis=AX.X, op=ALU.add), reads=[Rsq], writes=[Rst])
        c.op("act", lambda: nc.scalar.activation(out=st[:, 8:16], in_=st[:, 0:8], func=AF.Sqrt, bias=epsb[:, 0:1], scale=1.0 / 64),
             reads=[Rst, Reps], writes=[Rst])
        c.op("dve", lambda: nc.vector.reciprocal(out=st[:, 8:16], in_=st[:, 8:16]), reads=[Rst], writes=[Rst])
        c.op("dve", lambda: nc.vector.tensor_tensor(out=sq[:], in0=ot, in1=bc8(st[:, 8:16], 64), op=ALU.mult),
             reads=[Rof, Rst], writes=[Rsq])
        c.op("pool", lambda: nc.gpsimd.tensor_tensor(out=sq[:], in0=sq[:], in1=NW[:, :].unsqueeze(1).to_broadcast([128, 8, 64]), op=ALU.mult),
             reads=[Rsq, RNW], writes=[Rsq])
        c.op("act", lambda: nc.scalar.activation(out=zt[:], in_=zt[:], func=AF.Silu), reads=[Rzt], writes=[Rzt])
        c.op("dve", lambda: nc.vector.tensor_tensor(out=oa[:].rearrange("p (h e) -> p h e", e=64), in0=sq[:], in1=zt[:], op=ALU.mult),
             reads=[Rsq, Rzt], writes=[Roa])
        PT, RPT = PB.next()
        ptv = PT[:, :].bitcast(BF16)[:, 0:512].rearrange("p (k i) -> p k i", i=128)
        for k in range(4):
            c.op("pe", lambda: nc.tensor.transpose(ptv[:, k, :], oa[:, k * 128:(k + 1) * 128], self.identb[:]),
                 reads=[Roa, self.Ridb], writes=[RPT], pe_accum=True, sig=(k == 3))
        c.op("act", lambda: nc.scalar.copy(out=oT[:], in_=ptv), reads=[RPT], writes=[RoT])
        c.dma("sp", self.oaT_d.rearrange("(k p) s -> p k s", p=128)[:, :, t * 128:(t + 1) * 128], oT[:], reads=[RoT])


def _p4(self, l):
    c, nc, S = self.c, self.nc, self.S
    NT = S // 128
    AQ = c.sb("AQ", [128, 8, S], BF16); RAQ = Reg()
    AK = c.sb("AK", [128, 2, S], BF16); RAK = Reg()
    c.op("pool", lambda: nc.gpsimd.memset(AQ[64:128], 0.0), adds=[RAQ])
    c.op("pool", lambda: nc.gpsimd.memset(AK[64:128], 0.0), adds=[RAK])
    AV = c.sb("AV", [128, NT, 128], BF16); RAV = Reg()
    CB = c.sb("CB", [128, 3, 8, 128], BF16); RCB = Reg()
    c.dma("sp", AQ[0:64], self.aqT.rearrange("(h e) s -> e h s", e=64), adds=[RAQ])
    c.dma("sp", AK[0:64], self.akT.rearrange("(h e) s -> e h s", e=64), adds=[RAK])
    c.dma("sp", AV[:], self.avtok.rearrange("(t p) c -> p t c", p=128), writes=[RAV])
    c.dma("sp", CB[:], self.cb_d[:, :, :, :], writes=[RCB])
    ones = c.sb("ones", [128, 64], BF16); Rones = Reg()
    c.op("pool", lambda: nc.gpsimd.memset(ones[:], 1.0), writes=[Rones])
    ES = c.sb("ES", [64, 8], F32); RES = Reg()
    c.dma("sp", ES[:], self.attn_sink[l:l + 1, :].partition_broadcast(64), writes=[RES])
    c.op("act", lambda: nc.scalar.activation(out=ES[:], in_=ES[:], func=AF.Exp), reads=[RES], writes=[RES])
    OBT = c.sb("OBT", [128, 4, S], BF16); ROBT = Reg()
    psc = Rot(c, "psc", 3, [128, 512], F32, psum=True)
    ppv = Rot(c, "ppv", 2, [128, 512], F32, psum=True)
    pdn = Rot(c, "pdn", 2, [128, 512], F32, psum=True)
    PTs = Rot(c, "PT", 3, [128, 4, 128], BF16)
    dens = Rot(c, "den", 2, [64, 4, 128], F32)
    for n in range(NT):
        qs = slice(n * 128, (n + 1) * 128)
        for g in range(2):
            kts = [kt for kt in (n - 1, n, n + 1) if 0 <= kt < NT]
            pv, Rpv = ppv.next()
            dn, Rdn = pdn.next()
            pvv = pv[0:64, :].rearrange("p (h i) -> p h i", i=128)
            dnv = dn[0:64, :].rearrange("p (h i) -> p h i", i=128)
            for ki, kt in enumerate(kts):
                sc, Rsc = psc.next()
                scv = sc[:, :].rearrange("p (h i) -> p h i", i=128)
                c.op("pe", lambda: nc.tensor.matmul(scv, lhsT=AK[:, g, kt * 128:(kt + 1) * 128], rhs=AQ[:, g * 4:(g + 1) * 4, qs],
                                                    start=True, stop=False),
                     reads=[RAK, RAQ], writes=[Rsc], pe_accum=True, sig=False)
                c.op("pe", lambda: nc.tensor.matmul(scv, lhsT=self.identb[:], rhs=CB[:, kt - n + 1, g * 4:(g + 1) * 4, :],
                                                    start=False, stop=True),
                     reads=[RCB, self.Ridb], writes=[Rsc], pe_accum=True)
                PT, RPT = PTs.next()
                c.op("act", lambda: nc.scalar.activation(out=PT[:], in_=scv, func=AF.Exp, scale=0.125), reads=[Rsc], writes=[RPT])
                c.op("pe", lambda: nc.tensor.matmul(pvv, lhsT=AV[:, kt, g * 64:(g + 1) * 64], rhs=PT[:],
                                                    start=(ki == 0), stop=(ki == len(kts) - 1)),
                     reads=[RAV, RPT], writes=[Rpv], pe_accum=True, sig=(ki == len(kts) - 1))
                c.op("pe", lambda: nc.tensor.matmul(dnv, lhsT=ones[:], rhs=PT[:],
                                                    start=(ki == 0), stop=(ki == len(kts) - 1)),
                     reads=[Rones, RPT], writes=[Rdn], pe_accum=True, sig=(ki == len(kts) - 1))
            den, Rden = dens.next()
            c.op("dve", lambda: nc.vector.tensor_tensor(out=den[:], in0=dnv, in1=ES[:, g * 4:(g + 1) * 4].unsqueeze(2).to_broadcast([64, 4, 128]),
                                                        op=ALU.add),
                 reads=[Rdn, RES], writes=[Rden])
            c.op("dve", lambda: nc.vector.reciprocal(out=den[:], in_=den[:]), reads=[Rden], writes=[Rden])
            pv2 = pv[0:64, :].rearrange("p (a b i) -> p a b i", b=2, i=128)
            den2 = den[:].rearrange("p (a b) i -> p a b i", b=2)
            for par in range(2):
                c.op("dve", lambda: nc.vector.tensor_tensor(out=OBT[par * 64:(par + 1) * 64, g * 2:(g + 1) * 2, qs],
                                                            in0=pv2[:, :, par, :], in1=den2[:, :, par, :], op=ALU.mult),
                     reads=[Rpv, Rden], adds=[ROBT])
    c.dma("sp", self.obT_d.rearrange("(k p) s -> p k s", p=128), OBT[:], reads=[ROBT])


def _resid_epilogue(self, banks, xt, Rxt, GP, RGP, ss, Rss, k, dst, outs):
    c, nc = self.c, self.nc
    junk, Rj = self.junk
    for hf, (bk, Rbk) in enumerate(banks):
        c.op("act", lambda: nc.scalar.activation(out=junk[:, 0:512], in_=bk[:], func=AF.Square, accum_out=ss[:, 4 * k + hf:4 * k + hf + 1]),
             reads=[Rbk], writes=[Rj, Rss])
    c.op("dve", lambda: nc.vector.tensor_tensor(out=ss[:, 4 * k + 2:4 * k + 3], in0=ss[:, 4 * k:4 * k + 1], in1=ss[:, 4 * k + 1:4 * k + 2], op=ALU.add),
         reads=[Rss], writes=[Rss])
    self.rstd_of(ss[:, 4 * k + 2:4 * k + 3], ss[:, 4 * k + 3:4 * k + 4], Rss, D)
    xo, Rxo = outs.next()
    for hf, (bk, Rbk) in enumerate(banks):
        hs = slice(hf * 512, (hf + 1) * 512)
        c.op("dve", lambda: nc.vector.scalar_tensor_tensor(out=xo[:, hs], in0=bk[:], scalar=ss[:, 4 * k + 3:4 * k + 4], in1=GP[:, hs],
                                                           op0=ALU.mult, op1=ALU.mult),
             reads=[Rbk, Rss, RGP], adds=[Rxo])
    c.op("pool", lambda: nc.gpsimd.tensor_tensor(out=xo[:], in0=xo[:], in1=xt[:], op=ALU.add), reads=[Rxo, Rxt], writes=[Rxo])
    c.dma("sp", dst, xo[:], reads=[Rxo])
    Rxo.writers = [w for w in Rxo.writers[-1:]]


def _p5(self, l, Xin, Xout):
    c, nc, S = self.c, self.nc, self.S
    NB = S // 512
    self.epsb = c.sb("epsb5", [128, 1], F32); self.Reps = Reg()
    c.op("pool", lambda: nc.gpsimd.memset(self.epsb[:], EPS), writes=[self.Reps])
    WA = c.sb("WA", [128, 4, D], BF16); WB = c.sb("WB", [128, 4, D], BF16); WO = c.sb("WO", [128, 8, D], BF16)
    RWA = [[] for _ in range(4)]; RWB = [[] for _ in range(4)]; RWO = [[] for _ in range(8)]
    for kc in range(4):
        self.load_wcast(WA[:, kc, :], self.w_up_a[l, kc * 128:(kc + 1) * 128, :], RWA[kc], D)
        self.load_wcast(WB[:, kc, :], self.w_up_b[l, kc * 128:(kc + 1) * 128, :], RWB[kc], D)
    for kc in range(8):
        self.load_wcast(WO[:, kc, :], self.w_out[l, kc * 128:(kc + 1) * 128, :], RWO[kc], D)
    GP = c.sb("GP", [128, D], F32); RGP = Reg()
    c.dma("sp", GP[:], self.n_mix_post[l:l + 1, :].partition_broadcast(128), writes=[RGP])
    self.junk = (c.sb("junk5", [128, 512], BF16), Reg())
    ss = c.sb("ss5", [128, 4 * NB * 4], F32); Rss = Reg()
    c.op("pool", lambda: nc.gpsimd.memset(ss[:], 0.0), writes=[Rss])
    OAs = Rot(c, "OA", 2, [128, 4, 512], BF16); OBs = Rot(c, "OB", 2, [128, 4, 512], BF16)
    SGs = Rot(c, "SG", 2, [128, 16, 512], F32)
    MXs = Rot(c, "MX", 2, [128, 8, 512], BF16)
    t1s = Rot(c, "t1", 2, [128, 512], F32); t2s = Rot(c, "t2", 2, [128, 512], F32)
    xts = Rot(c, "xt5", 2, [128, D], F32); outs = Rot(c, "xo5", 2, [128, D], F32)
    pbk = Rot(c, "pb5", 8, [128, 512], F32, psum=True)
    for b in range(NB):
        bs = slice(b * 512, (b + 1) * 512)
        OA, ROA_ = OAs.next(); OB, ROB_ = OBs.next(); SG, RSG = SGs.next(); MX, RMX = MXs.next()
        c.dma("sp", OA[:], self.oaT_d.rearrange("(k p) s -> p k s", p=128)[:, :, bs], writes=[ROA_])
        c.dma("sp", OB[:], self.obT_d.rearrange("(k p) s -> p k s", p=128)[:, :, bs], writes=[ROB_])
        c.dma("sp", SG[:], self.sgT.rearrange("(k p) s -> p k s", p=128)[:, :, bs], writes=[RSG])
        RMX.writers = []
        for dc in range(8):
            ds_ = slice(dc * 128, (dc + 1) * 128)
            pa, Rpa = pbk.next()
            for kc in range(4):
                c.op("pe", lambda: nc.tensor.matmul(pa[:], lhsT=WA[:, kc, ds_], rhs=OA[:, kc, :], start=(kc == 0), stop=(kc == 3)),
                     reads=RWA[kc] + [ROA_], writes=[Rpa], pe_accum=True, sig=(kc == 3))
            t1, Rt1 = t1s.next()
            c.op("dve", lambda: nc.vector.tensor_tensor(out=t1[:], in0=pa[:], in1=SG[:, dc, :], op=ALU.mult), reads=[Rpa, RSG], writes=[Rt1])
            pb, Rpb = pbk.next()
            for kc in range(4):
                c.op("pe", lambda: nc.tensor.matmul(pb[:], lhsT=WB[:, kc, ds_], rhs=OB[:, kc, :], start=(kc == 0), stop=(kc == 3)),
                     reads=RWB[kc] + [ROB_], writes=[Rpb], pe_accum=True, sig=(kc == 3))
            t2, Rt2 = t2s.next()
            c.op("dve", lambda: nc.vector.tensor_tensor(out=t2[:], in0=pb[:], in1=SG[:, 8 + dc, :], op=ALU.mult), reads=[Rpb, RSG], writes=[Rt2])
            c.op("pool", lambda: nc.gpsimd.tensor_tensor(out=MX[:, dc, :], in0=t1[:], in1=t2[:], op=ALU.add), reads=[Rt1, Rt2], adds=[RMX])
        for tt in range(4):
            k = b * 4 + tt
            r0 = b * 512 + tt * 128
            xt, Rxt = xts.next()
            c.dma("sp", xt[:], Xin[r0:r0 + 128, :], writes=[Rxt])
            banks = []
            for hf in range(2):
                bk, Rbk = pbk.next()
                for kc in range(8):
                    c.op("pe", lambda: nc.tensor.matmul(bk[:], lhsT=MX[:, kc, tt * 128:(tt + 1) * 128], rhs=WO[:, kc, hf * 512:(hf + 1) * 512],
                                                        start=(kc == 0), stop=(kc == 7)),
                         reads=RWO[kc] + [RMX], writes=[Rbk], pe_accum=True, sig=(kc == 7))
                banks.append((bk, Rbk))
            self.resid_epilogue(banks, xt, Rxt, GP, RGP, ss, Rss, k, Xout[r0:r0 + 128, :], outs)


def _p6a(self, l, X):
    c, nc, S = self.c, self.nc, self.S
    NB = S // 512
    self.epsb = c.sb("epsb6", [128, 1], F32); self.Reps = Reg()
    c.op("pool", lambda: nc.gpsimd.memset(self.epsb[:], EPS), writes=[self.Reps])
    W1 = c.sb("W1", [128, 8, DFF], BF16)
    RW1 = [[] for _ in range(8)]
    for c0 in range(0, DFF, 1024):
        for kc in range(8):
            r = Reg("w")
            c.dma("pool", W1[:, kc, c0:c0 + 1024], self.w1[l, kc * 128:(kc + 1) * 128, c0:c0 + 1024], writes=[r])
            RW1[kc].append(r)
    gammaT = c.sb("gammaT6", [128, 8], F32); self.Rgam = Reg()
    with nc.allow_non_contiguous_dma(reason="tiny gamma transpose"):
        c.dma("sp", gammaT[:], self.n_mlp_pre.rearrange("l (c p) -> l p c", p=128)[l], writes=[self.Rgam])
    xts = Rot(c, "xt6", 2, [128, D], F32)
    self.junk = (c.sb("junk6", [128, D], BF16), Reg())
    self.xn = (c.sb("xn6", [128, D], BF16), Reg())
    ss = c.sb("ss6", [128, 2 * NB * 4], F32); Rss = Reg()
    c.op("pool", lambda: nc.gpsimd.memset(ss[:], 0.0), writes=[Rss])
    pT = c.ps("pT6", [128, 8, 128], BF16); RpT = Reg("pT6", excl=True)
    hTs = Rot(c, "hT6", 2, [128, 8, 512], BF16)
    pss = Rot(c, "pf6", 6, [128, 512], F32, psum=True)
    rls = Rot(c, "rl", 3, [128, 512], F32)
    ubs = Rot(c, "ub", 3, [128, 512], BF16)
    for b in range(NB):
        hT, RhT = hTs.next()
        RhT.writers = []
        for tt in range(4):
            self.norm_to_hT(X, b * 512 + tt * 128, l, gammaT, hT, RhT, tt, xts, ss, Rss, b * 4 + tt, pT, RpT)
        for fc in range(DFF // 128):
            ps, Rps = pss.next()
            for kc in range(8):
                c.op("pe", lambda: nc.tensor.matmul(ps[:], lhsT=W1[:, kc, fc * 128:(fc + 1) * 128], rhs=hT[:, kc, :],
                                                    start=(kc == 0), stop=(kc == 7)),
                     reads=RW1[kc] + [RhT], writes=[Rps], pe_accum=True, sig=(kc == 7))
            rl, Rrl = rls.next()
            ub, Rub = ubs.next()
            c.op("act", lambda: nc.scalar.activation(out=rl[:], in_=ps[:], func=AF.Relu), reads=[Rps], writes=[Rrl])
            e = "dve" if fc % 2 == 0 else "pool"
            eng = nc.vector if e == "dve" else nc.gpsimd
            c.op(e, lambda: eng.tensor_tensor(out=ub[:], in0=rl[:], in1=rl[:], op=ALU.mult), reads=[Rrl], writes=[Rub])
            c.dma("sp", self.uT_d[fc * 128:(fc + 1) * 128, b * 512:(b + 1) * 512], ub[:], reads=[Rub])


def _p6b(self, l, Xin, Xout):
    c, nc, S = self.c, self.nc, self.S
    NB = S // 512
    NF = DFF // 128
    self.epsb = c.sb("epsb7", [128, 1], F32); self.Reps = Reg()
    c.op("pool", lambda: nc.gpsimd.memset(self.epsb[:], EPS), writes=[self.Reps])
    W2 = c.sb("W2", [128, NF, D], BF16)
    RW2 = [[] for _ in range(NF)]
    for fc in range(NF):
        self.load_wcast(W2[:, fc, :], self.w2[l, fc * 128:(fc + 1) * 128, :], RW2[fc], D)
    GP = c.sb("GP7", [128, D], F32); RGP = Reg()
    c.dma("sp", GP[:], self.n_mlp_post[l:l + 1, :].partition_broadcast(128), writes=[RGP])
    self.junk = (c.sb("junk7", [128, 512], BF16), Reg())
    ss = c.sb("ss7", [128, 4 * NB * 4], F32); Rss = Reg()
    c.op("pool", lambda: nc.gpsimd.memset(ss[:], 0.0), writes=[Rss])
    UTs = Rot(c, "UT", 2, [128, NF, 512], BF16)
    xts = Rot(c, "xt7", 2, [128, D], F32); outs = Rot(c, "xo7", 2, [128, D], F32)
    pbk = Rot(c, "pb7", 6, [128, 512], F32, psum=True)
    for b in range(NB):
        UT, RUT = UTs.next()
        c.dma("sp", UT[:], self.uT_d.rearrange("(k p) s -> p k s", p=128)[:, :, b * 512:(b + 1) * 512], writes=[RUT])
        for tt in range(4):
            k = b * 4 + tt
            r0 = b * 512 + tt * 128
            xt, Rxt = xts.next()
            c.dma("sp", xt[:], Xin[r0:r0 + 128, :], writes=[Rxt])
            banks = []
            for hf in range(2):
                bk, Rbk = pbk.next()
                for fc in range(NF):
                    c.op("pe", lambda: nc.tensor.matmul(bk[:], lhsT=UT[:, fc, tt * 128:(tt + 1) * 128], rhs=W2[:, fc, hf * 512:(hf + 1) * 512],
                                                        start=(fc == 0), stop=(fc == NF - 1)),
                         reads=RW2[fc] + [RUT], writes=[Rbk], pe_accum=True, sig=(fc == NF - 1))
                banks.append((bk, Rbk))
            self.resid_epilogue(banks, xt, Rxt, GP, RGP, ss, Rss, k, Xout[r0:r0 + 128, :], outs)


Prog.p2a = _p2a
Prog.p2b = _p2b
Prog.p3 = _p3
Prog.p4 = _p4
Prog.p5 = _p5
Prog.p6a = _p6a
Prog.p6b = _p6b
Prog.resid_epilogue = _resid_epilogue


def build_prog(S, depth, dbg=(), stop=None):
    return Prog(S, depth, dbg, stop)


def make_in_map(inputs, xs, depth):
    cf, cb = host_consts()
    m = {"x": np.ascontiguousarray(xs), "cf": cf, "cb": cb}
    for k in ("w_in", "conv_w", "dn_norm_w", "attn_sink", "w_up_a", "w_up_b", "w_out", "norm_mix_pre",
              "norm_mix_post", "norm_mlp_pre", "norm_mlp_post", "w_mlp_in", "w_mlp_out"):
        m[k] = np.ascontiguousarray(np.asarray(inputs[k], np.float32)[:depth])
    m["a_log"] = np.ascontiguousarray(np.asarray(inputs["a_log"], np.float32)[:depth].reshape(depth, 16))
    m["dt_bias"] = np.ascontiguousarray(np.asarray(inputs["dt_bias"], np.float32)[:depth].reshape(depth, 16))
    return m


def kernel(**inputs):
    x = np.asarray(inputs["x"], np.float32)
    B, S, _ = x.shape
    depth = inputs["w_in"].shape[0]
    prog = build_prog(S, depth)
    in_maps = [make_in_map(inputs, x[b], depth) for b in range(B)]
    res = run_bass_kernel_spmd(prog.nc, in_maps, core_ids=list(range(B)))
    return np.stack([np.asarray(r["out"], np.float32) for r in res.results], axis=0)
```
